# Optimizing a Trainium2 kernel written in Bass

```python
import jax, jax.numpy as jnp
from jax import lax
import numpy as np

D_MODEL = 2048
BATCH = 2
SEQ = 8192
DEPTH = 1
DEC_BATCH = 16
DEC_SEQ = 2048
PAST_LEN = 128

D_POOL = D_MODEL // 2
POOL_WINDOWS = (2, 4, 8, 16)
N_POOL_GROUPS = len(POOL_WINDOWS)
POOL_GROUP = D_POOL // N_POOL_GROUPS
D_RWKV = D_MODEL // 2
HEAD_SIZE = 64
N_HEADS = D_RWKV // HEAD_SIZE
N_DIR = 2
LORA_W = 64
LORA_A = 64
LORA_G = 128
SHIFT_WIDTH = 3
D_FF = 5632
FFN_CONV_WIDTH = 3
NORM_EPS = 1e-6
LNX_EPS = 64e-5

D_RWKV_IN = 3 * D_RWKV + N_DIR * LORA_W + N_DIR * LORA_A + LORA_G
RWKV_SPLITS = (D_RWKV, 2 * D_RWKV, 3 * D_RWKV, 3 * D_RWKV + N_DIR * LORA_W,
               3 * D_RWKV + N_DIR * (LORA_W + LORA_A))
D_IN = D_POOL + D_RWKV_IN + 2 * D_MODEL
IN_SPLITS = (D_POOL, D_POOL + D_RWKV_IN, D_POOL + D_RWKV_IN + D_MODEL)

kernel_name = "pool_rwkv7_bidir_gated_encoder"


def _rmsnorm(x, g):
    xf = x.astype(jnp.float32)
    y = xf * lax.rsqrt(jnp.mean(xf * xf, axis=-1, keepdims=True) + NORM_EPS)
    return (y * g.astype(jnp.float32)).astype(x.dtype)


def _centred_dwconv(u, w):
    width = w.shape[0]
    pad = width // 2
    T = u.shape[1]
    up = jnp.pad(u, ((0, 0), (pad, pad), (0, 0)))
    out = up[:, 0:T] * w[0]
    for j in range(1, width):
        out = out + up[:, j:j + T] * w[j]
    return out


def _multiscale_pool(u, pool_w, pool_scale):
    B, T, _ = u.shape
    uf = u.astype(jnp.float32).reshape(B, T, N_POOL_GROUPS, POOL_GROUP)
    c = jnp.concatenate([jnp.zeros_like(uf[:, :1]), jnp.cumsum(uf, axis=1)], axis=1)
    half = jnp.array([w // 2 for w in POOL_WINDOWS], dtype=jnp.int32)
    t = jnp.arange(T, dtype=jnp.int32)[:, None]
    lo = jnp.clip(t - half, 0, T)
    hi = jnp.clip(t + half, 0, T)
    gi = jnp.arange(N_POOL_GROUPS, dtype=jnp.int32)[None, :]
    win_sum = c[:, hi, gi] - c[:, lo, gi]
    count = (hi - lo).astype(jnp.float32)[:, :, None]
    pooled = (win_sum / count - uf).astype(u.dtype)
    mixed = jnp.einsum('btgc,gcd->btgd', pooled, pool_w)
    return mixed.reshape(B, T, D_POOL) * pool_scale


def _wkv7_bidir(r, decay, k, v, kk, b):
    B, T, H, N = r.shape

    def both(z):
        return jnp.stack([z, z], axis=2)

    def to_scan(z):
        z = jnp.stack([z[:, :, 0], jnp.flip(z[:, :, 1], axis=1)], axis=0)
        return jnp.transpose(z, (2, 0, 1, 3, 4))

    xs = (to_scan(both(r)), to_scan(decay), to_scan(k), to_scan(both(v)), to_scan(both(kk)), to_scan(b))
    s0 = jnp.zeros((N_DIR, B, H, N, N), jnp.float32)

    def step(S, inp):
        r_t, w_t, k_t, v_t, kk_t, b_t = inp
        sa = jnp.einsum('zbhvk,zbhk->zbhv', S, kk_t)
        S = S * w_t[..., None, :] - sa[..., :, None] * b_t[..., None, :] + v_t[..., :, None] * k_t[..., None, :]
        return S, jnp.einsum('zbhvk,zbhk->zbhv', S, r_t)

    _, ys = lax.scan(step, s0, xs)
    ys = jnp.transpose(ys, (1, 2, 0, 3, 4))
    return ys[0] + jnp.flip(ys[1], axis=1)


def _rwkv_branch(u, w0, w_up, a0, a_up, g_up, k_k, k_a, r_k, lnx_w, lnx_b):
    B, T, _ = u.shape
    uf = u.astype(jnp.float32)
    r, k, v, wd, ad, gd = jnp.split(uf, RWKV_SPLITS, axis=-1)
    wd = wd.reshape(B, T, N_DIR, LORA_W)
    ad = ad.reshape(B, T, N_DIR, LORA_A)
    w_log = -jax.nn.softplus(-(w0 + jnp.einsum('btzl,zld->btzd', jnp.tanh(wd), w_up))) - 0.5
    decay = jnp.exp(-jnp.exp(w_log))
    a = jax.nn.sigmoid(a0 + jnp.einsum('btzl,zld->btzd', ad, a_up))
    g = jnp.einsum('btl,ld->btd', jax.nn.sigmoid(gd), g_up)
    kk = (k * k_k).reshape(B, T, N_HEADS, HEAD_SIZE)
    kk = kk / jnp.maximum(jnp.sqrt(jnp.sum(kk * kk, axis=-1, keepdims=True)), 1e-12)
    k_dir = k[:, :, None] * (1.0 + (a - 1.0) * k_a)
    b_dir = kk.reshape(B, T, 1, D_RWKV) * a

    def heads(z):
        return z.reshape(B, T, N_DIR, N_HEADS, HEAD_SIZE)

    rh = r.reshape(B, T, N_HEADS, HEAD_SIZE)
    vh = v.reshape(B, T, N_HEADS, HEAD_SIZE)
    kh = heads(k_dir)
    y = _wkv7_bidir(rh, heads(decay), kh, vh, kk, heads(b_dir))
    mu = jnp.mean(y, axis=-1, keepdims=True)
    var = jnp.mean(jnp.square(y - mu), axis=-1, keepdims=True)
    y = ((y - mu) * lax.rsqrt(var + LNX_EPS)).reshape(B, T, D_RWKV) * lnx_w + lnx_b
    bonus = jnp.sum(rh[:, :, None] * kh * r_k, axis=-1, keepdims=True) * vh[:, :, None]
    y = y + jnp.sum(bonus, axis=2).reshape(B, T, D_RWKV)
    return (y * g).astype(u.dtype)


def _layer(x, p):
    (norm_mix_g, w_in, shift_w, pool_w, pool_scale, p_pool, w0, w_up, a0, a_up, g_up, k_k, k_a, r_k,
     lnx_w, lnx_b, p_rwkv, w_out, norm_ffn_g, ffn_up, ffn_conv_w, ffn_conv_b, ffn_down) = p
    h = _rmsnorm(x, norm_mix_g)
    proj = jnp.einsum('btd,de->bte', h, w_in)
    u_pool, u_rwkv, gate_pool, gate_rwkv = jnp.split(proj, IN_SPLITS, axis=-1)
    pool_out = jnp.einsum('btc,cd->btd', _multiscale_pool(u_pool, pool_w, pool_scale), p_pool)
    u_rwkv = _centred_dwconv(u_rwkv, shift_w)
    rwkv_y = _rwkv_branch(u_rwkv, w0, w_up, a0, a_up, g_up, k_k, k_a, r_k, lnx_w, lnx_b)
    rwkv_out = jnp.einsum('btc,cd->btd', rwkv_y, p_rwkv)
    merged = jax.nn.sigmoid(gate_pool) * pool_out + jax.nn.sigmoid(gate_rwkv) * rwkv_out
    x = x + jnp.einsum('btd,de->bte', merged, w_out)
    h = _rmsnorm(x, norm_ffn_g)
    u = _centred_dwconv(jnp.einsum('btd,df->btf', h, ffn_up), ffn_conv_w) + ffn_conv_b
    u_gate, u_val = jnp.split(u, 2, axis=-1)
    x = x + jnp.einsum('btf,fd->btd', jax.nn.silu(u_gate) * u_val, ffn_down)
    return x


def _trunk(x, layer_params, norm_final_g):
    for l in range(DEPTH):
        x = _layer(x, [p[l] for p in layer_params])
    return _rmsnorm(x, norm_final_g)


def setup_inputs(seed: int = 0) -> dict:
    key = jax.random.key(seed)
    ks = jax.random.split(key, 32)
    f32 = jnp.float32

    def nrm(k, shape, scale):
        return jax.random.normal(k, shape, f32) * scale

    L = DEPTH
    shift_base = jnp.array([0.25, 0.5, 0.25], f32)[None, :, None]
    ffn_conv_base = jnp.array([0.2, 0.6, 0.2], f32)[None, :, None]
    return {
        "x_prompt": jax.random.normal(ks[0], (BATCH, SEQ, D_MODEL), f32),
        "x_sample": jax.random.normal(ks[1], (DEC_BATCH, DEC_SEQ, D_MODEL), f32),
        "norm_mix_g": 1.0 + nrm(ks[2], (L, D_MODEL), 0.05),
        "w_in": nrm(ks[3], (L, D_MODEL, D_IN), D_MODEL ** -0.5),
        "shift_w": shift_base + nrm(ks[4], (L, SHIFT_WIDTH, D_RWKV_IN), 0.05),
        "pool_w": nrm(ks[5], (L, N_POOL_GROUPS, POOL_GROUP, POOL_GROUP), POOL_GROUP ** -0.5),
        "pool_scale": 1.0 + nrm(ks[6], (L, D_POOL), 0.1),
        "p_pool": nrm(ks[7], (L, D_POOL, D_MODEL), D_POOL ** -0.5),
        "w0": jax.random.uniform(ks[8], (L, N_DIR, D_RWKV), f32, -5.0, 0.0),
        "w_up": nrm(ks[9], (L, N_DIR, LORA_W, D_RWKV), 0.5 * LORA_W ** -0.5),
        "a0": nrm(ks[10], (L, N_DIR, D_RWKV), 0.1),
        "a_up": nrm(ks[11], (L, N_DIR, LORA_A, D_RWKV), 0.5 * LORA_A ** -0.5),
        "g_up": nrm(ks[12], (L, LORA_G, D_RWKV), LORA_G ** -0.5),
        "k_k": 0.85 + nrm(ks[13], (L, D_RWKV), 0.05),
        "k_a": 1.0 + nrm(ks[14], (L, D_RWKV), 0.05),
        "r_k": nrm(ks[15], (L, N_DIR, N_HEADS, HEAD_SIZE), 0.1),
        "lnx_w": 1.0 + nrm(ks[16], (L, D_RWKV), 0.05),
        "lnx_b": nrm(ks[17], (L, D_RWKV), 0.02),
        "p_rwkv": nrm(ks[18], (L, D_RWKV, D_MODEL), D_RWKV ** -0.5),
        "w_out": nrm(ks[19], (L, D_MODEL, D_MODEL), D_MODEL ** -0.5),
        "norm_ffn_g": 1.0 + nrm(ks[20], (L, D_MODEL), 0.05),
        "ffn_up": nrm(ks[21], (L, D_MODEL, 2 * D_FF), D_MODEL ** -0.5),
        "ffn_conv_w": ffn_conv_base + nrm(ks[22], (L, FFN_CONV_WIDTH, 2 * D_FF), 0.05),
        "ffn_conv_b": nrm(ks[23], (L, 2 * D_FF), 0.02),
        "ffn_down": nrm(ks[24], (L, D_FF, D_MODEL), D_FF ** -0.5),
        "norm_final_g": 1.0 + nrm(ks[25], (D_MODEL,), 0.05),
    }


def reference(x_prompt, x_sample, norm_mix_g, w_in, shift_w, pool_w, pool_scale, p_pool, w0, w_up, a0, a_up,
              g_up, k_k, k_a, r_k, lnx_w, lnx_b, p_rwkv, w_out, norm_ffn_g, ffn_up, ffn_conv_w, ffn_conv_b,
              ffn_down, norm_final_g):
    layer_params = (norm_mix_g, w_in, shift_w, pool_w, pool_scale, p_pool, w0, w_up, a0, a_up, g_up, k_k, k_a,
                    r_k, lnx_w, lnx_b, p_rwkv, w_out, norm_ffn_g, ffn_up, ffn_conv_w, ffn_conv_b, ffn_down)
    y_prompt = _trunk(x_prompt, layer_params, norm_final_g)
    y_sample = _trunk(x_sample, layer_params, norm_final_g)
    return (y_prompt, y_sample)
```

```python
import numpy as np
from contextlib import ExitStack
import concourse.bass as bass
import concourse.mybir as mybir
from concourse.bass_utils import run_bass_kernel_spmd

F32 = mybir.dt.float32
BF16 = mybir.dt.bfloat16
ALU = mybir.AluOpType
AF = mybir.ActivationFunctionType
AX = mybir.AxisListType

D = 2048
DP = 1024
DR = 1024
DRIN = 3456
DIN = 8576
DFF = 5632
NCORES = 8
SCAN_CUT = 9
EPS = 1e-6
LNX_EPS = 64e-5
CDEC = -float(np.exp(-0.5))
POOL_HALF = (1, 2, 4, 8)


CV_LAYOUT = [("carry", [8]), ("norm_mix_g", [16]), ("norm_ffn_g", [16]), ("shift_w", [27, 3]), ("pool_scale", [8]),
             ("a0", [2, 8]), ("k_k", [8]), ("k_a", [8]), ("r_k", [2, 8]), ("lnx_w", [8]), ("lnx_b", [8]),
             ("ffn_conv_w", [88, 3]), ("ffn_conv_b", [88])]
CM_LAYOUT = [("tri0", [384]), ("tri1", [384]), ("msk0", [384]), ("msk1", [384]), ("ident", [128]), ("bones", [128])]


def _offsets(layout):
    off, o = {}, 0
    for nm, shp in layout:
        n = int(np.prod(shp))
        off[nm] = (o, shp)
        o += (n + 15) // 16 * 16
    return off, o


CV_OFF, CV_N = _offsets(CV_LAYOUT)
CM_OFF, CM_N = _offsets(CM_LAYOUT)


class Buf:
    __slots__ = ("t", "w", "r", "dsem", "dcnt", "name", "keep")

    def __init__(self, t=None, name=""):
        self.t = t
        self.w = {}
        self.r = {}
        self.dsem = None
        self.dcnt = 0
        self.name = name
        self.keep = False

    def __getitem__(self, k):
        return self.t[k]


class View(Buf):
    def __init__(self, parent, ap):
        self.t = ap
        self.w = parent.w
        self.r = parent.r
        self.dsem = None
        self.dcnt = 0
        self.name = "view"
        self.keep = True


class Sem:
    __slots__ = ("h",)

    def __init__(self, h):
        self.h = h


class Prog:
    EPOCH = 30000

    def __init__(self, nc, es):
        self.nc = nc
        self.es = es
        self.eng = {"pe": nc.tensor, "act": nc.scalar, "dve": nc.vector, "pool": nc.gpsimd, "sp": nc.sync}
        self.sem = {}
        self.cnt = {}
        self.seen = {e: {} for e in self.eng}
        self.nsem = 0
        self.allsems = []
        self.dead = set()
        self.recycled = []
        self._init_sems()
        for e in self.eng:
            self._new_epoch(e)
        self.dsems = []

    def _init_sems(self):
        self.free = []
        while True:
            try:
                self.free.append(self.nc.alloc_semaphore(name=f"sem{len(self.free)}"))
            except Exception:
                break
        for h in self.free:
            self.nc.gpsimd.sem_clear(h)
        self.nc.all_engine_barrier()

    def _newsem(self, name):
        self.nsem += 1
        h = Sem(self.free.pop())
        self.allsems.append(h)
        return h

    def _new_epoch(self, e):
        self.sem[e] = self._newsem("e" + e)
        self.cnt[e] = 0

    def release(self, dummy):
        rel = [b for b in self.dsems if not b.keep]
        import os as _os
        if "keepall" in _os.environ.get("LWV", ""):
            rel = []
        if not rel:
            return
        self.nc.all_engine_barrier()
        for b in rel:
            self.nc.gpsimd.sem_clear(b.dsem.h)
            self.dead.add(id(b.dsem))
            self.free.append(b.dsem.h)
            b.dsem = None
            b.dcnt = 0
        self.dsems = [b for b in self.dsems if b.keep]
        self.nc.all_engine_barrier()

    def _wait(self, e, deps):
        need = {}
        for s, c in deps:
            k = id(s)
            if k in self.dead:
                continue
            if k not in need or need[k][1] < c:
                need[k] = (s, c)
        for k, (s, c) in need.items():
            if e == "pe" and s is self.sem["pe"]:
                continue
            if self.seen[e].get(k, 0) >= c:
                continue
            self.eng[e].wait_ge(s.h, c)
            self.seen[e][k] = c

    @staticmethod
    def _add(d, tok):
        k = id(tok[0])
        if k not in d or d[k][1] < tok[1]:
            d[k] = tok

    def op(self, e, fn, reads=(), writes=(), inc=True):
        deps = []
        for b in reads:
            deps.extend(b.w.values())
        for b in writes:
            deps.extend(b.w.values())
            deps.extend(b.r.values())
        self._wait(e, deps)
        ins = fn(self.eng[e])
        if inc:
            self.cnt[e] += 1
            ins.then_inc(self.sem[e].h, 1)
            tok = (self.sem[e], self.cnt[e])
        else:
            tok = (self.sem[e], self.cnt[e] + 1)
        for b in reads:
            self._add(b.r, tok)
        for b in writes:
            b.w = {id(tok[0]): tok}
            b.r = {}
        if inc and self.cnt[e] >= self.EPOCH:
            self._new_epoch(e)
        return ins

    def dma(self, out_ap, in_ap, reads=(), writes=(), q="sp", owner=None, part=False):
        if owner is None:
            owner = writes[0] if writes and writes[0].t is not None else reads[0]
        if owner.dsem is None:
            owner.dsem = self._newsem("d")
            self.dsems.append(owner)
        deps = []
        for b in reads:
            deps.extend(b.w.values())
        for b in writes:
            for tok in b.w.values():
                if part and tok[0] is owner.dsem:
                    continue
                deps.append(tok)
            deps.extend(b.r.values())
        self._wait(q, deps)
        owner.dcnt += 16
        self.eng[q].dma_start(out=out_ap, in_=in_ap).then_inc(owner.dsem.h, 16)
        tok = (owner.dsem, owner.dcnt)
        import os as _os
        if "serial" in _os.environ.get("LWV", ""):
            self._wait(q, [tok])
        for b in reads:
            self._add(b.r, tok)
        for b in writes:
            if part:
                self._add(b.w, tok)
            else:
                b.w = {id(tok[0]): tok}
                b.r = {}

    def barrier(self):
        toks = [(self.sem[e], self.cnt[e]) for e in self.eng if self.cnt[e] > 0]
        toks += [(b.dsem, b.dcnt) for b in self.dsems if b.dcnt > 0]
        for e in self.eng:
            self._wait(e, toks)

    def finish(self):
        deps = [(b.dsem, b.dcnt) for b in self.dsems if b.dcnt > 0]
        self._wait("sp", deps)


class Ring:
    def __init__(self, bufs):
        self.bufs = bufs
        self.i = 0

    def next(self):
        b = self.bufs[self.i % len(self.bufs)]
        self.i += 1
        return b


def build(SL=2048, NSLOT=4, debug=(), upto=9, skip_front=False):
    TT = SL * NSLOT
    TB = 512
    NB = TT // TB
    BPS = SL // TB
    nc = bass.Bass("TRN2", target_bir_lowering=False)
    es = ExitStack()
    P = Prog(nc, es)
    gstack = ExitStack()
    es.enter_context(gstack)

    def din(name, shape, dt=F32):
        return nc.dram_tensor(name, list(shape), dt, kind="ExternalInput").ap()

    def dscr(name, shape, dt=F32):
        kind = "ExternalOutput" if name in debug else "Internal"
        return nc.dram_tensor(name, list(shape), dt, kind=kind).ap()

    uid = [0]

    def sb(stack, name, shape, dt=F32, n=1, ring=False):
        uid[0] += 1
        name = f"{name}u{uid[0]}_"
        nbytes = int(np.prod(shape[1:])) * (2 if dt == BF16 else 4)
        bufs = []
        for i in range(n):
            bufs.append(Buf(stack.enter_context(nc.sbuf_tensor(f"{name}{i}", list(shape), dt)), f"{name}{i}"))
            if nbytes % 64 != 0:
                padb = 64 - (((nbytes + 31) // 32 * 32) % 64)
                if padb != 64:
                    stack.enter_context(nc.sbuf_tensor(f"{name}{i}pad", [128, padb // 4], F32))
        for b_ in bufs:
            b_.keep = stack is gstack
        return Ring(bufs) if (n > 1 or ring) else bufs[0]

    def ps(stack, name, shape, dt=F32, n=1):
        uid[0] += 1
        name = f"{name}u{uid[0]}_"
        bufs = [Buf(stack.enter_context(nc.psum_tensor(f"{name}{i}", list(shape), dt)), f"{name}{i}") for i in range(n)]
        return bufs[0] if n == 1 else Ring(bufs)

    x_in = din("x", [TT, D])
    y_out = nc.dram_tensor("y", [TT, D], F32, kind="ExternalOutput").ap()
    class LazyIn(dict):
        def __init__(self, shapes):
            super().__init__()
            self.shapes = dict(shapes)

        def __missing__(self, nm):
            self[nm] = din(nm, self.shapes[nm])
            return self[nm]

    W = LazyIn([("w_in", [D, DIN]), ("p_pool", [DP, D]), ("p_rwkv", [DR, D]), ("w_out", [D, D]),
                    ("ffn_up", [D, 2 * DFF]), ("ffn_down", [DFF, D]), ("pool_w", [4, 256, 256]),
                    ("w_up", [128, DR]), ("a_up", [128, DR]), ("g_up", [128, DR]),
                    ("norm_mix_g", [128, 16]), ("norm_ffn_g", [128, 16]), ("norm_final_g", [128, D]),
                    ("shift_w", [128, 27, 3]), ("pool_scale", [128, 8]), ("w0", [2, 128, DR]), ("a0", [128, 2, 8]),
                    ("k_k", [128, 8]), ("k_a", [128, 8]), ("r_k", [128, 2, 8]), ("lnx_w", [128, 8]), ("lnx_b", [128, 8]),
                    ("ffn_conv_w", [128, 88, 3]), ("ffn_conv_b", [128, 88]),
                    ("invcnt", [4, 128, TT]), ("carry", [128, NSLOT + 1]),
                    ("cvec", [128, CV_N]), ("cmat", [128, CM_N])])

    def wscr(name, K, N):
        KC = K // 128
        NG = (N + 511) // 512
        return dict(ap=dscr("s_" + name, [NG, 128, KC, 512], BF16), KC=KC, NG=NG, N=N,
                    bufs=[Buf(None, f"{name}_g{g}") for g in range(NG)])

    WS = {nm: wscr(nm, k, n) for nm, k, n in [("w_in", D, DIN), ("p_pool", DP, D), ("p_rwkv", DR, D), ("w_out", D, D),
                                                ("ffn_up", D, 2 * DFF), ("ffn_down", DFF, D)]}
    if skip_front:
        PROJ_A = din("s_proj", [35 * 128, TT])
        PROJ_G = din("s_projg", [32 * 128, TT])
    else:
        PROJ_A = dscr("s_proj", [35 * 128, TT])
        PROJ_G = dscr("s_projg", [32 * 128, TT])

    def prow(j):
        return PROJ_A[j * 128:(j + 1) * 128, :] if j < 35 else PROJ_G[(j - 35) * 128:(j - 34) * 128, :]
    PROJ_B = [[Buf(None, f"proj{j}_{b}") for b in range(NB)] for j in range(DIN // 128)]

    cvec_t = sb(gstack, "cvec", [128, CV_N])
    cmat_t = sb(gstack, "cmat", [128, CM_N])
    P.dma(cvec_t[:], W["cvec"], writes=[cvec_t])
    P.dma(cmat_t[:], W["cmat"], writes=[cmat_t])

    def cview(tile, offs, nm):
        o, shp = offs[nm]
        n = int(np.prod(shp))
        ap = tile[:, o:o + n]
        if len(shp) == 2:
            ap = ap.rearrange("p (a b) -> p a b", b=shp[1])
        return View(tile, ap)

    CVW = {nm: cview(cvec_t, CV_OFF, nm) for nm in CV_OFF}
    CMW = {nm: cview(cmat_t, CM_OFF, nm) for nm in CM_OFF}
    ident_f = CMW["ident"]
    gmix = CVW["norm_mix_g"]
    ident_b = sb(gstack, "identb", [128, 128], BF16)
    eps_t = sb(gstack, "eps", [128, 16])
    dummy = sb(gstack, "dummy", [128, 16])
    P.op("dve", lambda e: e.tensor_copy(ident_b[:], ident_f[:]), reads=[ident_f], writes=[ident_b])
    P.op("dve", lambda e: e.memset(eps_t[:], EPS), writes=[eps_t])

    import os as _os0
    if "early" in _os0.environ.get("LWV", ""):
        e_f = sb(gstack, "earlyf", [128, DR])
        e_b = sb(gstack, "earlyb", [128, DR], BF16)
        P.dma(e_f[:], W["w_up"], writes=[e_f])
        P.op("dve", lambda e: e.tensor_copy(e_b[:], e_f[:]), reads=[e_f], writes=[e_b])

    def convert_weights(names):
        with ExitStack() as st:
            stg = sb(st, "cvf", [128, 8, 512], F32, n=2)
            stb = sb(st, "cvb", [128, 8, 512], BF16, n=2)
            k = 0
            for nm in names:
                ws = WS[nm]
                src = W[nm].rearrange("(kc p) n -> p kc n", p=128)
                for g in range(ws["NG"]):
                    ncol = min(512, ws["N"] - g * 512)
                    for k0 in range(0, ws["KC"], 8):
                        kn = min(8, ws["KC"] - k0)
                        f = stg.next()
                        b = stb.next()
                        P.dma(f[:, 0:kn, 0:ncol], src[:, k0:k0 + kn, g * 512:g * 512 + ncol], writes=[f])
                        e = ("pool", "dve", "act")[k % 3]
                        k += 1
                        if e == "act":
                            P.op(e, lambda en: en.copy(b[:, 0:kn, 0:ncol], f[:, 0:kn, 0:ncol]), reads=[f], writes=[b])
                        else:
                            P.op(e, lambda en: en.tensor_copy(b[:, 0:kn, 0:ncol], f[:, 0:kn, 0:ncol]), reads=[f], writes=[b])
                        P.dma(ws["ap"][g, :, k0:k0 + kn, 0:ncol], b[:, 0:kn, 0:ncol], reads=[b], writes=[ws["bufs"][g]],
                              owner=b, part=True)
            P.barrier()
            P.release(dummy)

    if not skip_front:
        convert_weights(["w_in"])

    def stage1():
        with ExitStack() as st:
            xt = sb(st, "xt", [128, D], F32, n=2)
            junk = sb(st, "junk", [128, D], BF16, n=1)
            xs = sb(st, "xs", [128, D], BF16, n=2)
            ss = sb(st, "ss", [128, 1], F32, n=2)
            rstd = sb(st, "rstd", [128, 1], F32, n=2)
            hT = sb(st, "hT", [128, 16, TB], BF16, n=2)
            wt = sb(st, "wt", [128, 16, 512], BF16, n=3)
            og = sb(st, "og", [128, TB], F32, n=4)
            ptr = ps(st, "ptr", [128, 16, 128], BF16, n=1)
            pg = ps(st, "pg", [128, TB], F32, n=4)
            ws = WS["w_in"]
            k = 0
            for b in range(NB):
                h = hT.next()
                for i in range(TB // 128):
                    t0 = b * TB + i * 128
                    x = xt.next()
                    P.dma(x[:], x_in[t0:t0 + 128, :], writes=[x])
                    s_ = ss.next()
                    r_ = rstd.next()
                    xs_ = xs.next()
                    P.op("act", lambda e: e.activation(junk[:], x[:], AF.Square, accum_out=s_[:]), reads=[x], writes=[junk, s_])
                    P.op("dve", lambda e: e.tensor_scalar(r_[:], s_[:], 1.0 / D, EPS, ALU.mult, ALU.add), reads=[s_], writes=[r_])
                    P.op("act", lambda e: e.sqrt(r_[:], r_[:]), reads=[r_], writes=[r_])
                    P.op("dve", lambda e: e.reciprocal(r_[:], r_[:]), reads=[r_], writes=[r_])
                    P.op("act", lambda e: e.activation(xs_[:], x[:], AF.Copy, scale=r_[:]), reads=[x, r_], writes=[xs_])
                    for kc in range(16):
                        P.op("pe", lambda e: e.transpose(ptr[:, kc, :], xs_[:, kc * 128:(kc + 1) * 128], ident_b[:]),
                             reads=[xs_, ident_b], writes=[ptr], inc=(kc == 15))
                    P.op("dve", lambda e: e.tensor_tensor(h[:, :, i * 128:(i + 1) * 128], ptr[:],
                                                          gmix[:].unsqueeze(2).to_broadcast([128, 16, 128]), ALU.mult),
                         reads=[ptr, gmix], writes=[h])
                for g in range(ws["NG"]):
                    w_ = wt.next()
                    ncol = min(512, ws["N"] - g * 512)
                    P.dma(w_[:, :, 0:ncol], ws["ap"][g, :, :, 0:ncol], reads=[ws["bufs"][g]], writes=[w_])
                    for jj in range(ncol // 128):
                        j = g * 4 + jj
                        pt = pg.next()
                        for kc in range(16):
                            P.op("pe", lambda e: e.matmul(pt[:], w_[:, kc, jj * 128:(jj + 1) * 128], h[:, kc, :],
                                                          start=(kc == 0), stop=(kc == 15)),
                                 reads=[w_, h], writes=[pt], inc=(kc == 15))
                        o = og.next()
                        gate = j >= (DP + DRIN) // 128
                        if gate:
                            P.op("act", lambda e: e.activation(o[:], pt[:], AF.Sigmoid), reads=[pt], writes=[o])
                        elif k % 2 == 0:
                            P.op("dve", lambda e: e.tensor_copy(o[:], pt[:]), reads=[pt], writes=[o])
                        else:
                            P.op("act", lambda e: e.copy(o[:], pt[:]), reads=[pt], writes=[o])
                        k += 1
                        P.dma(prow(j)[:, b * TB:(b + 1) * TB], o[:], reads=[o], writes=[PROJ_B[j][b]], owner=o)
            P.barrier()
            P.release(dummy)

    if not skip_front:
        stage1()

    carry_t = CVW["carry"]

    def load_halo(dst, rows_ap, bufs_row, tok0, n, hl):
        lo = max(tok0 - hl, 0)
        hi = min(tok0 + n + hl, TT)
        if lo > tok0 - hl:
            P.op("pool", lambda e: e.memset(dst[:, 0:hl], 0.0), writes=[dst])
        if hi < tok0 + n + hl:
            P.op("pool", lambda e: e.memset(dst[:, hl + n:hl + n + hl], 0.0), writes=[dst])
        deps = [bufs_row[bb] for bb in range(lo // TB, (hi - 1) // TB + 1)]
        P.dma(dst[:, lo - (tok0 - hl):hi - (tok0 - hl)], rows_ap[:, lo:hi], reads=deps, writes=[dst], part=True)
        if tok0 % SL == 0 and tok0 > 0:
            s = tok0 // SL
            P.op("pool", lambda e: e.tensor_scalar(dst[:, 0:hl], dst[:, 0:hl], carry_t[:, s:s + 1], None, ALU.mult),
                 reads=[carry_t], writes=[dst])
        if (tok0 + n) % SL == 0 and tok0 + n < TT:
            s = (tok0 + n) // SL
            P.op("pool", lambda e: e.tensor_scalar(dst[:, hl + n:hl + n + hl], dst[:, hl + n:hl + n + hl],
                                                   carry_t[:, s:s + 1], None, ALU.mult), reads=[carry_t], writes=[dst])

    def conv3(e, out, src, wt3, c, n, reads, tmp=None):
        P.op(e, lambda en: en.tensor_scalar(out[:, 0:n], src[:, 0:n], wt3[:, c, 0:1], None, ALU.mult),
             reads=[src, wt3] + reads, writes=[out])
        for j in (1, 2):
            if e == "dve":
                P.op(e, lambda en: en.scalar_tensor_tensor(out[:, 0:n], src[:, j:j + n], wt3[:, c, j:j + 1], out[:, 0:n],
                                                           ALU.mult, ALU.add), reads=[src, wt3], writes=[out])
            else:
                P.op(e, lambda en: en.tensor_scalar(tmp[:, 0:n], src[:, j:j + n], wt3[:, c, j:j + 1], None, ALU.mult),
                     reads=[src, wt3], writes=[tmp])
                P.op(e, lambda en: en.tensor_tensor(out[:, 0:n], out[:, 0:n], tmp[:, 0:n], ALU.add), reads=[tmp], writes=[out])

    YF = dscr("s_yf", [TT, DR])
    YF_B = [[Buf(None, f"yf{b}_{hp}") for hp in range(8)] for b in range(NB)]
    RY = dscr("s_ry", [DR, TT], BF16)
    RY_B = [[Buf(None, f"ry{hp}_{b}") for b in range(NB)] for hp in range(8)]

    def scan_pass(dr):
        od = 1 - dr
        with ExitStack() as st:
            tri = CMW[f"tri{dr}"]
            msk = CMW[f"msk{dr}"]
            bones = CMW["bones"]
            shw = CVW["shift_w"]
            vec = {nm: CVW[nm] for nm in ("k_k", "k_a", "lnx_w", "lnx_b", "a0", "r_k")}
            if SCAN_CUT == 0:
                P.barrier()
                P.release(dummy)
                return
            omka = sb(st, "omka", [128, 8])
            import os as _os3
            if "noomka" not in _os3.environ.get("LWV", ""):
                P.op("dve", lambda e: e.tensor_scalar(omka[:], vec["k_a"][:], -1.0, 1.0, ALU.mult, ALU.add), reads=[vec["k_a"]], writes=[omka])
            if SCAN_CUT == -2:
                P.barrier()
                P.release(dummy)
                return
            w0b = sb(st, "w0b", [128, DR])
            import os as _os2
            if "now0" not in _os2.environ.get("LWV", ""):
                P.dma(w0b[:], W["w0"][dr], writes=[w0b])
            if SCAN_CUT == -3:
                P.barrier()
                P.release(dummy)
                return
            lw = {nm: sb(st, "lw_" + nm, [128, DR], BF16) for nm in ("w_up", "a_up", "g_up")}
            with ExitStack() as st2:
                tmpf = sb(st2, "lwf", [128, DR])
                import os as _os
                _v = _os.environ.get("LWV", "")
                for nm in ("w_up", "a_up", "g_up")[:(1 if "one" in _v else 3)]:
                    if "v9" in _v:
                        P.dma(tmpf[:, 0:128], W["c_ident"], writes=[tmpf])
                        P.op("dve", lambda e: e.tensor_copy(lw[nm][:, 0:128], tmpf[:, 0:128]), reads=[tmpf], writes=[lw[nm]])
                        continue
                    if "v6" in _v:
                        P.dma(tmpf[:, 0:384], W["c_tri"][1], writes=[tmpf])
                        P.op("dve", lambda e: e.tensor_copy(lw[nm][:, 0:384], tmpf[:, 0:384]), reads=[tmpf], writes=[lw[nm]])
                        continue
                    if "v5" in _v:
                        P.dma(tmpf[:, 0:512], W[nm][:, 0:512], writes=[tmpf], part=True)
                        P.dma(tmpf[:, 512:1024], W[nm][:, 512:1024], writes=[tmpf], part=True)
                    else:
                        P.dma(tmpf[:], W[nm], writes=[tmpf])
                    if "nocast" not in _v:
                        if "v3" in _v:
                            P.op("dve", lambda e: e.memset(lw["a_up"][:], 1.0), writes=[lw["a_up"]])
                            P.op("dve", lambda e: e.tensor_copy(lw[nm][:], lw["a_up"][:]), reads=[lw["a_up"]], writes=[lw[nm]])
                        elif "v4" in _v:
                            P.op("dve", lambda e: e.tensor_copy(omka[:], w0b[:, 0:8]), reads=[w0b], writes=[omka])
                        elif "v1" in _v:
                            P.op("dve", lambda e: e.tensor_copy(lw[nm][:], w0b[:]), reads=[w0b], writes=[lw[nm]])
                        elif "v2" in _v:
                            P.op("dve", lambda e: e.tensor_copy(w0b[:, 0:512], tmpf[:, 0:512]), reads=[tmpf], writes=[w0b])
                        elif "act" in _v:
                            P.op("act", lambda e: e.copy(lw[nm][:], tmpf[:]), reads=[tmpf], writes=[lw[nm]])
                        elif "pool" in _v:
                            P.op("pool", lambda e: e.tensor_copy(lw[nm][:], tmpf[:]), reads=[tmpf], writes=[lw[nm]])
                        elif "half" in _v:
                            P.op("dve", lambda e: e.tensor_copy(lw[nm][:, 0:512], tmpf[:, 0:512]), reads=[tmpf], writes=[lw[nm]])
                        else:
                            P.op("dve", lambda e: e.tensor_copy(lw[nm][:], tmpf[:]), reads=[tmpf], writes=[lw[nm]])
                P.barrier()
            if "norel" not in _v:
                P.release(dummy)
            if SCAN_CUT == -1:
                P.barrier()
                P.release(dummy)
                return
            S_t = [st.enter_context(nc.sbuf_tensor(f"S{hp}d{dr}", [128, 64], BF16)) for hp in range(8)]
            S_b = [[Buf(S_t[hp], f"S{hp}_{hh}") for hh in range(2)] for hp in range(8)]
            for hp in range(8):
                for hh in range(2):
                    P.op("pool", lambda e: e.memset(S_t[hp][hh * 64:(hh + 1) * 64, :], 0.0), writes=[S_b[hp][hh]])
            HB = sb(st, "halo", [128, TB + 2], F32, n=4)
            ctmp = sb(st, "ctmp", [128, TB], F32)
            cw = sb(st, "cw", [128, TB], F32, n=3)
            tw = sb(st, "tw", [128, TB], BF16, n=2)
            adb = sb(st, "adb", [128, TB], BF16, n=2)
            sgd = sb(st, "sgd", [128, TB], BF16, n=2)
            F = {nm: sb(st, "f_" + nm, [128, TB], F32, n=2) for nm in
                 ("r", "k", "v", "a", "ao", "sq", "rn", "kk", "t", "kd", "kdo", "b", "e1", "e2", "e3", "e4", "dd", "m", "yc", "bon")}
            Bf = {nm: sb(st, "b_" + nm, [128, TB], BF16, n=2) for nm in ("at", "rt", "bt", "kt", "bh", "kh", "vb", "ry")}
            sgw = sb(st, "sgw", [128, 4, 128], F32, n=2)
            wend = sb(st, "wend", [128, 8], F32, n=2)
            Tk = {nm: sb(st, "t_" + nm, [128, 4, 128], BF16, n=2) for nm in ("A", "Bh", "Kh", "V")}
            ATs = sb(st, "ATs", [128, 4, 128], BF16, n=8)
            Nt = sb(st, "Nt", [128, 128], BF16, n=8)
            Xl = sb(st, "Xl", [128, 2, 128], BF16, n=40)
            Zr = sb(st, "Zr", [128, 128], BF16, n=24)
            Y0 = sb(st, "Y0", [128, 64], F32, n=8)
            YGT_t = [st.enter_context(nc.sbuf_tensor(f"YGT{i}d{dr}", [128, 128], BF16)) for i in range(4)]
            PT_t = [st.enter_context(nc.sbuf_tensor(f"PT{i}d{dr}", [128, 128], BF16)) for i in range(4)]
            Q_t = [st.enter_context(nc.sbuf_tensor(f"Qt{i}d{dr}", [128, 128], F32)) for i in range(4)]
            ytile = sb(st, "ytile", [128, 4, 128], F32, n=2)
            yfl = sb(st, "yfl", [128, 4, 128], F32, n=2)
            ysq = sb(st, "ysq", [128, 4, 128], F32, n=2)
            ynb = sb(st, "ynb", [128, 4, 128], BF16, n=2)
            st8 = {nm: sb(st, "s8" + nm, [128, 8], F32, n=2) for nm in ("s1", "s2", "mu", "var")}
            pA = ps(st, "pA", [128, 512], F32, n=2)
            pT = ps(st, "pT", [128, 4, 128], BF16, n=2)
            pU = ps(st, "pU", [128, 512], F32, n=4)
            ring_i = [0]

            slots = range(NSLOT) if dr == 0 else range(NSLOT - 1, -1, -1)
            if SCAN_CUT <= 1:
                slots = []
            for s in slots:
                first = (s == 0) if dr == 0 else (s == NSLOT - 1)
                if not first:
                    cs = s if dr == 0 else s + 1
                    for hp in range(8):
                        for hh in range(2):
                            P.op("pool", lambda e: e.tensor_scalar(S_t[hp][hh * 64:(hh + 1) * 64, :], S_t[hp][hh * 64:(hh + 1) * 64, :],
                                                                   carry_t[hh * 64:(hh + 1) * 64, cs:cs + 1], None, ALU.mult),
                                 reads=[carry_t], writes=[S_b[hp][hh]])
                blocks = range(BPS) if dr == 0 else range(BPS - 1, -1, -1)
                for bi in blocks:
                    b = s * BPS + bi
                    tok0 = b * TB
                    cws = []
                    for c in (24, 25, 26):
                        hb = HB.next()
                        load_halo(hb, prow(8 + c), PROJ_B[8 + c], tok0, TB, 1)
                        o = cw.next()
                        conv3("pool", o, hb, shw, c, TB, [], tmp=ctmp)
                        cws.append(o)
                    tw_ = tw.next(); adb_ = adb.next(); sgd_ = sgd.next()
                    P.op("act", lambda e: e.activation(tw_[:], cws[0][:], AF.Tanh), reads=[cws[0]], writes=[tw_])
                    P.op("dve", lambda e: e.tensor_copy(adb_[:], cws[1][:]), reads=[cws[1]], writes=[adb_])
                    P.op("act", lambda e: e.activation(sgd_[:], cws[2][:], AF.Sigmoid), reads=[cws[2]], writes=[sgd_])
                    for hp in range(8):
                        hs = slice(hp * 128, (hp + 1) * 128)
                        f = {k: v.next() for k, v in F.items()}
                        bq = {k: v.next() for k, v in Bf.items()}
                        for nm, c in (("r", hp), ("k", 8 + hp), ("v", 16 + hp)):
                            hb = HB.next()
                            load_halo(hb, prow(8 + c), PROJ_B[8 + c], tok0, TB, 1)
                            conv3("dve" if nm != "v" else "pool", f[nm], hb, shw, c, TB, [], tmp=ctmp)
                        def a_of(d_, dst):
                            p_ = pA.next()
                            P.op("pe", lambda e: e.matmul(p_[:], lw["a_up"][d_ * 64:(d_ + 1) * 64, hs], adb_[d_ * 64:(d_ + 1) * 64, :],
                                                          start=True, stop=True), reads=[lw["a_up"], adb_], writes=[p_])
                            P.op("act", lambda e: e.activation(dst[:], p_[:], AF.Sigmoid, bias=vec["a0"][:, d_, hp:hp + 1]),
                                 reads=[p_, vec["a0"]], writes=[dst])
                        a_of(dr, f["a"])
                        p_ = pA.next()
                        for i in range(4):
                            P.op("pe", lambda e: e.matmul(p_[:, i * 128:(i + 1) * 128], tw_[dr * 64:(dr + 1) * 64, i * 128:(i + 1) * 128],
                                                          lw["w_up"][dr * 64:(dr + 1) * 64, hs], start=True, stop=True),
                                 reads=[tw_, lw["w_up"]], writes=[p_], inc=(i == 3))
                        sgw_ = sgw.next()
                        P.op("dve", lambda e: e.tensor_tensor(sgw_[:], p_[:].rearrange("p (i c) -> p i c", i=4),
                                                              w0b[:, hs].unsqueeze(1).to_broadcast([128, 4, 128]), ALU.add),
                             reads=[p_, w0b], writes=[sgw_])
                        P.op("act", lambda e: e.activation(sgw_[:], sgw_[:], AF.Sigmoid), reads=[sgw_], writes=[sgw_])
                        def cum(v):
                            p2 = pA.next()
                            for i in range(4):
                                P.op("pe", lambda e: e.matmul(p2[:, i * 128:(i + 1) * 128], sgw_[:, i, :], tri[:, v * 128:(v + 1) * 128],
                                                              start=True, stop=True), reads=[sgw_, tri], writes=[p2], inc=(i == 3))
                            return p2
                        pc = cum(0)
                        P.op("act", lambda e: e.activation(f["e1"][:], pc[:], AF.Exp), reads=[pc], writes=[f["e1"]])
                        P.op("act", lambda e: e.activation(f["e3"][:], pc[:], AF.Exp, scale=-1.0), reads=[pc], writes=[f["e3"]])
                        P.op("dve", lambda e: e.tensor_copy(f["dd"][:], pc[:]), reads=[pc], writes=[f["dd"]])
                        pc = cum(1)
                        P.op("act", lambda e: e.activation(f["e2"][:], pc[:], AF.Exp), reads=[pc], writes=[f["e2"]])
                        pc = cum(2)
                        wend_ = wend.next()
                        P.op("act", lambda e: e.activation(wend_[:], pc[:].rearrange("p (j c) -> p j c", c=64)[:, :, 0], AF.Exp),
                             reads=[pc], writes=[wend_])
                        P.op("dve", lambda e: e.tensor_tensor(f["dd"][:], pc[:], f["dd"][:], ALU.subtract), reads=[pc, f["dd"]], writes=[f["dd"]])
                        P.op("act", lambda e: e.activation(f["e4"][:], f["dd"][:], AF.Exp), reads=[f["dd"]], writes=[f["e4"]])
                        P.op("act", lambda e: e.activation(f["sq"][:], f["k"][:], AF.Square, scale=vec["k_k"][:, hp:hp + 1]),
                             reads=[f["k"], vec["k_k"]], writes=[f["sq"]])
                        p3 = pA.next()
                        P.op("pe", lambda e: e.matmul(p3[:], bones[:], f["sq"][:], start=True, stop=True), reads=[bones, f["sq"]], writes=[p3])
                        P.op("act", lambda e: e.sqrt(f["rn"][:], p3[:]), reads=[p3], writes=[f["rn"]])
                        P.op("dve", lambda e: e.tensor_scalar(f["rn"][:], f["rn"][:], 1e-12, None, ALU.max), reads=[f["rn"]], writes=[f["rn"]])
                        P.op("dve", lambda e: e.reciprocal(f["rn"][:], f["rn"][:]), reads=[f["rn"]], writes=[f["rn"]])
                        P.op("dve", lambda e: e.scalar_tensor_tensor(f["kk"][:], f["k"][:], vec["k_k"][:, hp:hp + 1], f["rn"][:], ALU.mult, ALU.mult),
                             reads=[f["k"], vec["k_k"], f["rn"]], writes=[f["kk"]])
                        def kdir(asrc, dst):
                            P.op("dve", lambda e: e.tensor_scalar(f["t"][:], asrc[:], vec["k_a"][:, hp:hp + 1], omka[:, hp:hp + 1], ALU.mult, ALU.add),
                                 reads=[asrc, vec["k_a"], omka], writes=[f["t"]])
                            P.op("dve", lambda e: e.tensor_tensor(dst[:], f["k"][:], f["t"][:], ALU.mult), reads=[f["k"], f["t"]], writes=[dst])
                        kdir(f["a"], f["kd"])
                        P.op("pool", lambda e: e.tensor_tensor(f["b"][:], f["kk"][:], f["a"][:], ALU.mult), reads=[f["kk"], f["a"]], writes=[f["b"]])
                        P.op("dve", lambda e: e.scalar_tensor_tensor(bq["at"][:], f["kk"][:], -1.0, f["e2"][:], ALU.mult, ALU.mult),
                             reads=[f["kk"], f["e2"]], writes=[bq["at"]])
                        P.op("pool", lambda e: e.tensor_tensor(bq["rt"][:], f["r"][:], f["e1"][:], ALU.mult), reads=[f["r"], f["e1"]], writes=[bq["rt"]])
                        P.op("dve", lambda e: e.tensor_tensor(bq["bt"][:], f["b"][:], f["e3"][:], ALU.mult), reads=[f["b"], f["e3"]], writes=[bq["bt"]])
                        P.op("pool", lambda e: e.tensor_tensor(bq["kt"][:], f["kd"][:], f["e3"][:], ALU.mult), reads=[f["kd"], f["e3"]], writes=[bq["kt"]])
                        P.op("dve", lambda e: e.tensor_tensor(bq["bh"][:], f["b"][:], f["e4"][:], ALU.mult), reads=[f["b"], f["e4"]], writes=[bq["bh"]])
                        P.op("pool", lambda e: e.tensor_tensor(bq["kh"][:], f["kd"][:], f["e4"][:], ALU.mult), reads=[f["kd"], f["e4"]], writes=[bq["kh"]])
                        P.op("act", lambda e: e.copy(bq["vb"][:], f["v"][:]), reads=[f["v"]], writes=[bq["vb"]])
                        tk = {}
                        for k2, (nm, src) in enumerate((("A", "at"), ("Bh", "bh"), ("Kh", "kh"), ("V", "vb"))):
                            pt_ = pT.next()
                            for i in range(4):
                                P.op("pe", lambda e: e.transpose(pt_[:, i, :], bq[src][:, i * 128:(i + 1) * 128], ident_b[:]),
                                     reads=[bq[src], ident_b], writes=[pt_], inc=(i == 3))
                            tk[nm] = Tk[nm].next()
                            P.op("act" if k2 % 2 else "dve",
                                 (lambda e: e.copy(tk[nm][:], pt_[:])) if k2 % 2 else (lambda e: e.tensor_copy(tk[nm][:], pt_[:])),
                                 reads=[pt_], writes=[tk[nm]])
                        if SCAN_CUT <= 2:
                            continue
                        yt_ = ytile.next()
                        yt_parts = [[Buf(yt_.t, "ytp") for hh in range(2)] for i in range(4)]
                        for i in range(4):
                            for hh in range(2):
                                yt_parts[i][hh].w = dict(yt_.w); yt_parts[i][hh].r = dict(yt_.r)

                        def unit(i, hh, slot_idx):
                            cs_ = slice(hh * 64, (hh + 1) * 64)
                            ts_ = slice(i * 128, (i + 1) * 128)
                            Sb = S_b[hp][hh]
                            St = S_t[hp]
                            ats = ATs.next()
                            pu = pU.next()
                            for n2, (l_, r_) in enumerate((("bt", "at"), ("bt", "rt"), ("kt", "at"), ("kt", "rt"))):
                                P.op("pe", lambda e: e.matmul(pu[:, n2 * 128:(n2 + 1) * 128], bq[l_][cs_, ts_], bq[r_][cs_, ts_], start=True, stop=True),
                                     reads=[bq[l_], bq[r_]], writes=[pu], inc=(n2 == 3))
                            P.op("dve", lambda e: e.tensor_tensor(ats[:].rearrange("p (a b) c -> p a (b c)", a=2),
                                                                  pu[:].rearrange("p (a c) -> p a c", a=2),
                                                                  msk[:, 0:256].unsqueeze(1).to_broadcast([128, 2, 256]), ALU.mult),
                                 reads=[pu, msk], writes=[ats])
                            pu = pU.next()
                            P.op("pe", lambda e: e.matmul(pu[:, 0:128], bq["at"][cs_, ts_], bq["bt"][cs_, ts_], start=True, stop=True),
                                 reads=[bq["at"], bq["bt"]], writes=[pu])
                            nt = Nt.next()
                            P.op("act", lambda e: e.activation(nt[:], pu[:, 0:128], AF.Copy), reads=[pu], writes=[nt])
                            P.op("pool", lambda e: e.tensor_tensor(nt[:], nt[:], msk[:, 256:384], ALU.mult), reads=[msk], writes=[nt])
                            yield
                            z = Zr.next()
                            pu = pU.next()
                            P.op("pe", lambda e: e.matmul(pu[:, 0:64], ats[:, 2, :], tk["V"][:, i, cs_], start=True, stop=True),
                                 reads=[ats, tk["V"]], writes=[pu])
                            P.op("act", lambda e: e.copy(z[:, 64:128], pu[:, 0:64]), reads=[pu], writes=[z])
                            P.op("pool", lambda e: e.tensor_copy(z[:, 0:64], tk["A"][:, i, cs_]), reads=[tk["A"]], writes=[z])
                            yield
                            X = [None] * 6
                            XT = [None] * 6
                            X[0] = (nt, nt[:]); XT[0] = (ats, ats[:, 0, :])
                            for j in range(1, 6):
                                pu = pU.next()
                                if j < 5:
                                    P.op("pe", lambda e: e.matmul(pu[:, 0:128], XT[j - 1][1], X[j - 1][1], start=True, stop=True),
                                         reads=[XT[j - 1][0], X[j - 1][0]], writes=[pu], inc=False)
                                P.op("pe", lambda e: e.matmul(pu[:, 128:256], X[j - 1][1], XT[j - 1][1], start=True, stop=True),
                                     reads=[XT[j - 1][0], X[j - 1][0]], writes=[pu])
                                xl = Xl.next()
                                if j < 5:
                                    P.op("act" if j % 2 else "dve",
                                         (lambda e: e.copy(xl[:].rearrange("p a c -> p (a c)"), pu[:, 0:256])) if j % 2 else
                                         (lambda e: e.tensor_copy(xl[:].rearrange("p a c -> p (a c)"), pu[:, 0:256])),
                                         reads=[pu], writes=[xl])
                                else:
                                    P.op("act", lambda e: e.copy(xl[:, 1, :], pu[:, 128:256]), reads=[pu], writes=[xl])
                                X[j] = (xl, xl[:, 0, :]); XT[j] = (xl, xl[:, 1, :])
                                yield
                            for j in range(5, -1, -1):
                                pu = pU.next()
                                P.op("pe", lambda e: e.matmul(pu[:, 0:128], XT[j][1], z[:], start=True, stop=True),
                                     reads=[XT[j][0], z], writes=[pu])
                                zn = Zr.next()
                                P.op("dve", lambda e: e.tensor_tensor(zn[:], pu[:, 0:128], z[:], ALU.add), reads=[pu, z], writes=[zn])
                                z = zn
                                yield
                            G = z[:, 0:64]
                            U0 = z[:, 64:128]
                            pu = pU.next()
                            P.op("pe", lambda e: e.matmul(pu[:, 0:64], ats[:, 1, :], U0, start=True, stop=False), reads=[ats, z], writes=[pu], inc=False)
                            P.op("pe", lambda e: e.matmul(pu[:, 0:64], ats[:, 3, :], tk["V"][:, i, cs_], start=False, stop=True), reads=[ats, tk["V"]], writes=[pu])
                            y0 = Y0.next()
                            P.op("act", lambda e: e.copy(y0[:], pu[:, 0:64]), reads=[pu], writes=[y0])
                            pu = pU.next()
                            pub = pU.next()
                            pq = [pu, pub]
                            P.op("pe", lambda e: e.matmul(pu[cs_, 0:128], G, ats[:, 1, :], start=True, stop=True), reads=[z, ats], writes=[pu], inc=False)
                            for q in range(2):
                                qs = slice(q * 64, (q + 1) * 64)
                                P.op("pe", lambda e: e.matmul(pq[q][cs_, 128 + q * 64:192 + q * 64], z[qs, 0:64], tk["Bh"][qs, i, cs_], start=True, stop=True),
                                     reads=[z, tk["Bh"]], writes=[pq[q]], inc=True)
                            ygt = Buf(YGT_t[slot_idx % 4], "ygt"); ptb = Buf(PT_t[slot_idx % 4], "pt"); qb = Buf(Q_t[slot_idx % 4], "q")
                            ygt.w = dict(prev[slot_idx % 4][hh]); ptb.w = dict(prev[slot_idx % 4][hh]); qb.w = dict(prev[slot_idx % 4][hh])
                            P.op("dve", lambda e: e.tensor_tensor(ygt.t[cs_, :], pu[cs_, 0:128], bq["rt"][cs_, ts_], ALU.add), reads=[pu, bq["rt"]], writes=[ygt])
                            for q in range(2):
                                P.op("dve", lambda e: e.scalar_tensor_tensor(ptb.t[cs_, q * 64:(q + 1) * 64], ident_f[cs_, cs_],
                                                                             wend_[cs_, 2 * i + q:2 * i + q + 1], pq[q][cs_, 128 + q * 64:192 + q * 64],
                                                                             ALU.mult, ALU.add), reads=[ident_f, wend_, pq[q]], writes=[ptb])
                            pq = [pU.next(), pU.next()]
                            for q in range(2):
                                qs = slice(q * 64, (q + 1) * 64)
                                P.op("pe", lambda e: e.matmul(pq[q][cs_, q * 64:(q + 1) * 64], tk["Bh"][qs, i, cs_], z[qs, 64:128], start=True, stop=False),
                                     reads=[tk["Bh"], z], writes=[pq[q]], inc=False)
                                P.op("pe", lambda e: e.matmul(pq[q][cs_, q * 64:(q + 1) * 64], tk["Kh"][qs, i, cs_], tk["V"][qs, i, cs_], start=False, stop=True),
                                     reads=[tk["Kh"], tk["V"]], writes=[pq[q]], inc=True)
                                P.op("act", lambda e: e.copy(qb.t[cs_, q * 64:(q + 1) * 64], pq[q][cs_, q * 64:(q + 1) * 64]), reads=[pq[q]], writes=[qb])
                            yield
                            puy = pU.next()
                            for q in ((0, 1) if dr == 0 else (1, 0)):
                                qs = slice(q * 64, (q + 1) * 64)
                                P.op("pe", lambda e: e.matmul(puy[qs, 0:64], ygt.t[cs_, qs], St[cs_, :], start=True, stop=True),
                                     reads=[ygt, Sb], writes=[puy])
                                pus = pU.next()
                                P.op("pe", lambda e: e.matmul(pus[cs_, 0:64], ptb.t[cs_, qs], St[cs_, :], start=True, stop=True),
                                     reads=[ptb, Sb], writes=[pus])
                                P.op("dve", lambda e: e.tensor_tensor(St[cs_, :], pus[cs_, 0:64], qb.t[cs_, qs], ALU.add),
                                     reads=[pus, qb], writes=[Sb])
                            P.op("dve", lambda e: e.tensor_tensor(yt_.t[:, i, cs_], puy[:, 0:64], y0[:], ALU.add),
                                 reads=[puy, y0], writes=[yt_parts[i][hh]])
                            d = {}
                            for bb_ in (ygt, ptb, qb):
                                for tok in list(bb_.r.values()) + list(bb_.w.values()):
                                    Prog._add(d, tok)
                            prev[slot_idx % 4][hh] = d
                            yield

                        tiles = range(4) if dr == 0 else range(3, -1, -1)
                        gens = []
                        for ui, i in enumerate(tiles):
                            for hh in range(2):
                                gens.append(unit(i, hh, ui))
                        alive = list(gens)
                        while alive:
                            nxt = []
                            for g_ in alive:
                                try:
                                    next(g_)
                                    nxt.append(g_)
                                except StopIteration:
                                    pass
                            alive = nxt
                        yt_.w = {}
                        yt_.r = {}
                        for i in range(4):
                            for hh in range(2):
                                for tok in yt_parts[i][hh].w.values():
                                    Prog._add(yt_.w, tok)
                        t0 = tok0
                        if dr == 0:
                            P.dma(YF[t0:t0 + TB, hs].rearrange("(i p) c -> p i c", p=128), yt_[:], reads=[yt_], writes=[YF_B[b][hp]], owner=yt_)
                            continue
                        yf_ = yfl.next()
                        P.dma(yf_[:], YF[t0:t0 + TB, hs].rearrange("(i p) c -> p i c", p=128), reads=[YF_B[b][hp]], writes=[yf_])
                        P.op("pool", lambda e: e.tensor_tensor(yt_[:], yt_[:], yf_[:], ALU.add), reads=[yf_], writes=[yt_])
                        s8 = {k: v.next() for k, v in st8.items()}
                        ysq_ = ysq.next(); ynb_ = ynb.next()
                        v8 = lambda ap: ap.rearrange("p i (h c) -> p (i h) c", h=2)
                        P.op("dve", lambda e: e.tensor_reduce(s8["s1"][:], v8(yt_[:]), axis=AX.X, op=ALU.add), reads=[yt_], writes=[s8["s1"]])
                        P.op("act", lambda e: e.activation(ysq_[:], yt_[:], AF.Square), reads=[yt_], writes=[ysq_])
                        P.op("dve", lambda e: e.tensor_reduce(s8["s2"][:], v8(ysq_[:]), axis=AX.X, op=ALU.add), reads=[ysq_], writes=[s8["s2"]])
                        P.op("dve", lambda e: e.tensor_scalar(s8["mu"][:], s8["s1"][:], 1.0 / 64, None, ALU.mult), reads=[s8["s1"]], writes=[s8["mu"]])
                        P.op("dve", lambda e: e.tensor_tensor(s8["var"][:], s8["mu"][:], s8["mu"][:], ALU.mult), reads=[s8["mu"]], writes=[s8["var"]])
                        P.op("dve", lambda e: e.scalar_tensor_tensor(s8["var"][:], s8["s2"][:], 1.0 / 64, s8["var"][:], ALU.mult, ALU.subtract),
                             reads=[s8["s2"]], writes=[s8["var"]])
                        P.op("dve", lambda e: e.tensor_scalar(s8["var"][:], s8["var"][:], LNX_EPS, None, ALU.add), writes=[s8["var"]])
                        P.op("act", lambda e: e.sqrt(s8["var"][:], s8["var"][:]), writes=[s8["var"]])
                        P.op("dve", lambda e: e.reciprocal(s8["var"][:], s8["var"][:]), writes=[s8["var"]])
                        P.op("dve", lambda e: e.tensor_tensor(v8(yt_[:]), v8(yt_[:]), s8["mu"][:].unsqueeze(2).to_broadcast([128, 8, 64]), ALU.subtract),
                             reads=[s8["mu"]], writes=[yt_])
                        P.op("dve", lambda e: e.tensor_tensor(v8(ynb_[:]), v8(yt_[:]), s8["var"][:].unsqueeze(2).to_broadcast([128, 8, 64]), ALU.mult),
                             reads=[yt_, s8["var"]], writes=[ynb_])
                        pt_ = pT.next()
                        for i in range(4):
                            P.op("pe", lambda e: e.transpose(pt_[:, i, :], ynb_[:, i, :], ident_b[:]), reads=[ynb_, ident_b], writes=[pt_], inc=(i == 3))
                        P.op("dve", lambda e: e.tensor_scalar(f["yc"][:], pt_[:].rearrange("p i c -> p (i c)"), vec["lnx_w"][:, hp:hp + 1],
                                                              vec["lnx_b"][:, hp:hp + 1], ALU.mult, ALU.add),
                             reads=[pt_, vec["lnx_w"], vec["lnx_b"]], writes=[f["yc"]])
                        a_of(od, f["ao"])
                        kdir(f["ao"], f["kdo"])
                        P.op("pool", lambda e: e.tensor_scalar(f["m"][:], f["kd"][:], vec["r_k"][:, dr, hp:hp + 1], None, ALU.mult),
                             reads=[f["kd"], vec["r_k"]], writes=[f["m"]])
                        P.op("dve", lambda e: e.scalar_tensor_tensor(f["m"][:], f["kdo"][:], vec["r_k"][:, od, hp:hp + 1], f["m"][:], ALU.mult, ALU.add),
                             reads=[f["kdo"], vec["r_k"]], writes=[f["m"]])
                        P.op("pool", lambda e: e.tensor_tensor(f["m"][:], f["m"][:], f["r"][:], ALU.mult), reads=[f["r"]], writes=[f["m"]])
                        p4 = pA.next()
                        P.op("pe", lambda e: e.matmul(p4[:], bones[:], f["m"][:], start=True, stop=True), reads=[bones, f["m"]], writes=[p4])
                        P.op("dve", lambda e: e.tensor_tensor(f["bon"][:], p4[:], f["v"][:], ALU.mult), reads=[p4, f["v"]], writes=[f["bon"]])
                        P.op("pool", lambda e: e.tensor_tensor(f["yc"][:], f["yc"][:], f["bon"][:], ALU.add), reads=[f["bon"]], writes=[f["yc"]])
                        p5 = pA.next()
                        P.op("pe", lambda e: e.matmul(p5[:], lw["g_up"][:, hs], sgd_[:], start=True, stop=True), reads=[lw["g_up"], sgd_], writes=[p5])
                        P.op("dve", lambda e: e.tensor_tensor(bq["ry"][:], p5[:], f["yc"][:], ALU.mult), reads=[p5, f["yc"]], writes=[bq["ry"]])
                        P.dma(RY[hs, t0:t0 + TB], bq["ry"][:], reads=[bq["ry"]], writes=[RY_B[hp][b]], owner=bq["ry"])
            P.barrier()
            P.release(dummy)

    prev = [[{}, {}] for _ in range(4)]
    if upto >= 2:
        scan_pass(0)
    prev = [[{}, {}] for _ in range(4)]
    if upto >= 3:
        scan_pass(1)

    X1 = dscr("s_x1", [TT, D])
    X1_B = [Buf(None, f"x1_{t}") for t in range(TT // 128)]
    H2T = dscr("s_h2t", [D, TT], BF16)
    H2T_B = [Buf(None, f"h2t_{b}") for b in range(NB)]
    gffn = CVW["norm_ffn_g"]

    def rms_rstd(x, s_, r_, junk):
        P.op("act", lambda e: e.activation(junk[:], x[:], AF.Square, accum_out=s_[:]), reads=[x], writes=[junk, s_])
        P.op("dve", lambda e: e.tensor_scalar(r_[:], s_[:], 1.0 / D, EPS, ALU.mult, ALU.add), reads=[s_], writes=[r_])
        P.op("act", lambda e: e.sqrt(r_[:], r_[:]), reads=[r_], writes=[r_])
        P.op("dve", lambda e: e.reciprocal(r_[:], r_[:]), reads=[r_], writes=[r_])

    def stage4():
        convert_weights(["p_pool", "p_rwkv", "w_out"])
        with ExitStack() as st:
            pws = sb(st, "pws", [128, 4, 2, 256], BF16)
            psc = CVW["pool_scale"]
            with ExitStack() as st2:
                pwf = sb(st2, "pwf", [128, 4, 2, 256])
                P.dma(pwf[:], W["pool_w"].rearrange("g (cc p) d -> p g cc d", p=128), writes=[pwf])
                P.op("dve", lambda e: e.tensor_copy(pws[:], pwf[:]), reads=[pwf], writes=[pws])
                P.barrier()
            P.release(dummy)
            wpp = sb(st, "wpp", [128, 8, 512], BF16, n=1, ring=True)
            wpr = sb(st, "wpr", [128, 8, 512], BF16, n=1, ring=True)
            wo = sb(st, "wo", [128, 16, 512], BF16, n=2)
            WD = TB + 16
            hbp = sb(st, "hbp", [128, WD], F32, n=2)
            acc = [sb(st, f"acc{k}", [128, WD], F32, n=1, ring=True) for k in range(4)]
            invc = [sb(st, f"invc{g}", [128, TB], F32, n=1) for g in range(4)]
            pooled = [sb(st, f"pooled{c}", [128, TB], BF16, n=1) for c in range(8)]
            tmpw = sb(st, "tmpw", [128, TB], F32, n=2)
            mixed = [sb(st, f"mixed{c}", [128, TB], BF16, n=1) for c in range(8)]
            ryb = [sb(st, f"ryb{c}", [128, TB], BF16, n=1) for c in range(8)]
            gate = sb(st, "gate", [128, TB], F32, n=2)
            mp = sb(st, "mp", [128, TB], F32, n=2)
            merged = sb(st, "merged", [128, 16, TB], BF16, n=1)
            xt = sb(st, "x4", [128, D], F32, n=4)
            junk = sb(st, "junk4", [128, D], BF16, n=1)
            xs = sb(st, "xs4", [128, D], BF16, n=1, ring=True)
            ss = sb(st, "ss4", [128, 1], F32, n=2)
            rstd = sb(st, "rstd4", [128, 1], F32, n=2)
            h2 = sb(st, "h2", [128, 16, TB], BF16, n=1, ring=True)
            pg = ps(st, "pg4", [128, 512], F32, n=5)
            ptr = ps(st, "ptr4", [128, 16, 128], BF16, n=1)
            for b in range(NB):
                tok0 = b * TB
                for g in range(4):
                    P.dma(invc[g][:], W["invcnt"][g, :, tok0:tok0 + TB], writes=[invc[g]])
                for c in range(8):
                    g = c // 2
                    hb = hbp.next()
                    load_halo(hb, prow(c), PROJ_B[c], tok0, TB, 8)
                    a = [acc[k].next() for k in range(4)]
                    eng = "pool" if c % 2 else "dve"
                    P.op(eng, lambda e: e.tensor_tensor(a[0][:, 1:WD], hb[:, 0:WD - 1], hb[:, 1:WD], ALU.add), reads=[hb], writes=[a[0]])
                    if g >= 1:
                        P.op(eng, lambda e: e.tensor_tensor(a[1][:, 2:WD - 1], a[0][:, 1:WD - 2], a[0][:, 3:WD], ALU.add), reads=[a[0]], writes=[a[1]])
                    if g >= 2:
                        P.op(eng, lambda e: e.tensor_tensor(a[2][:, 4:WD - 3], a[1][:, 2:WD - 5], a[1][:, 6:WD - 1], ALU.add), reads=[a[1]], writes=[a[2]])
                    if g >= 3:
                        P.op(eng, lambda e: e.tensor_tensor(a[3][:, 8:WD - 7], a[2][:, 4:WD - 11], a[2][:, 12:WD - 3], ALU.add), reads=[a[2]], writes=[a[3]])
                    win = a[g]
                    t_ = tmpw.next()
                    P.op(eng, lambda e: e.tensor_tensor(t_[:], win[:, 8:8 + TB], invc[g][:], ALU.mult), reads=[win, invc[g]], writes=[t_])
                    P.op(eng, lambda e: e.tensor_tensor(pooled[c][:], t_[:], hb[:, 8:8 + TB], ALU.subtract), reads=[t_, hb], writes=[pooled[c]])
                for dc in range(8):
                    g, dd = dc // 2, dc % 2
                    p_ = pg.next()
                    for cc in range(2):
                        P.op("pe", lambda e: e.matmul(p_[:], pws[:, g, cc, dd * 128:(dd + 1) * 128], pooled[2 * g + cc][:], start=(cc == 0), stop=(cc == 1)),
                             reads=[pws, pooled[2 * g + cc]], writes=[p_], inc=(cc == 1))
                    P.op("act", lambda e: e.activation(mixed[dc][:], p_[:], AF.Copy, scale=psc[:, dc:dc + 1]), reads=[p_, psc], writes=[mixed[dc]])
                for c in range(8):
                    P.dma(ryb[c][:], RY[c * 128:(c + 1) * 128, tok0:tok0 + TB], reads=[RY_B[c][b]], writes=[ryb[c]])
                for g4 in range(4):
                    wp_ = wpp.next(); wr_ = wpr.next()
                    P.dma(wp_[:], WS["p_pool"]["ap"][g4], reads=[WS["p_pool"]["bufs"][g4]], writes=[wp_])
                    P.dma(wr_[:], WS["p_rwkv"]["ap"][g4], reads=[WS["p_rwkv"]["bufs"][g4]], writes=[wr_])
                    for jj in range(4):
                        e_ = g4 * 4 + jj
                        gp = gate.next(); gr = gate.next()
                        P.dma(gp[:], prow(35 + e_)[:, tok0:tok0 + TB], reads=[PROJ_B[35 + e_][b]], writes=[gp])
                        P.dma(gr[:], prow(51 + e_)[:, tok0:tok0 + TB], reads=[PROJ_B[51 + e_][b]], writes=[gr])
                        p1 = pg.next()
                        for c in range(8):
                            P.op("pe", lambda e: e.matmul(p1[:], wp_[:, c, jj * 128:(jj + 1) * 128], mixed[c][:], start=(c == 0), stop=(c == 7)),
                                 reads=[wp_, mixed[c]], writes=[p1], inc=(c == 7))
                        p2 = pg.next()
                        for c in range(8):
                            P.op("pe", lambda e: e.matmul(p2[:], wr_[:, c, jj * 128:(jj + 1) * 128], ryb[c][:], start=(c == 0), stop=(c == 7)),
                                 reads=[wr_, ryb[c]], writes=[p2], inc=(c == 7))
                        m_ = mp.next()
                        P.op("dve", lambda e: e.tensor_tensor(m_[:], p1[:], gp[:], ALU.mult), reads=[p1, gp], writes=[m_])
                        P.op("dve", lambda e: e.tensor_tensor(gr[:], p2[:], gr[:], ALU.mult), reads=[p2], writes=[gr])
                        P.op("pool", lambda e: e.tensor_tensor(merged[:, e_, :], m_[:], gr[:], ALU.add), reads=[m_, gr], writes=[merged])
                h = h2.next()
                xl = []
                for i in range(4):
                    t0 = tok0 + i * 128
                    x = xt.next()
                    P.dma(x[:], x_in[t0:t0 + 128, :], writes=[x])
                    xl.append(x)
                for n in range(4):
                    w_ = wo.next()
                    P.dma(w_[:], WS["w_out"]["ap"][n], reads=[WS["w_out"]["bufs"][n]], writes=[w_])
                    for i in range(4):
                        x = xl[i]
                        p_ = pg.next()
                        for kc in range(16):
                            P.op("pe", lambda e: e.matmul(p_[:], merged[:, kc, i * 128:(i + 1) * 128], w_[:, kc, :], start=(kc == 0), stop=(kc == 15)),
                                 reads=[merged, w_], writes=[p_], inc=(kc == 15))
                        P.op("dve", lambda e: e.tensor_tensor(x[:, n * 512:(n + 1) * 512], p_[:], x[:, n * 512:(n + 1) * 512], ALU.add), reads=[p_], writes=[x])
                for i in range(4):
                    t0 = tok0 + i * 128
                    x = xl[i]
                    P.dma(X1[t0:t0 + 128, :], x[:], reads=[x], writes=[X1_B[t0 // 128]], owner=x)
                    s_ = ss.next(); r_ = rstd.next(); xs_ = xs.next()
                    rms_rstd(x, s_, r_, junk)
                    P.op("act", lambda e: e.activation(xs_[:], x[:], AF.Copy, scale=r_[:]), reads=[x, r_], writes=[xs_])
                    for kc in range(16):
                        P.op("pe", lambda e: e.transpose(ptr[:, kc, :], xs_[:, kc * 128:(kc + 1) * 128], ident_b[:]),
                             reads=[xs_, ident_b], writes=[ptr], inc=(kc == 15))
                    P.op("dve", lambda e: e.tensor_tensor(h[:, :, i * 128:(i + 1) * 128], ptr[:],
                                                          gffn[:].unsqueeze(2).to_broadcast([128, 16, 128]), ALU.mult),
                         reads=[ptr, gffn], writes=[h])
                P.dma(H2T.rearrange("(kc p) t -> p kc t", p=128)[:, :, tok0:tok0 + TB], h[:], reads=[h], writes=[H2T_B[b]], owner=h)
            P.barrier()
            P.release(dummy)

    if upto >= 4:
        stage4()

    def stage5():
        convert_weights(["ffn_up", "ffn_down"])
        with ExitStack() as st:
            fcw = CVW["ffn_conv_w"]
            fcb = CVW["ffn_conv_b"]
            gfin = sb(st, "gfin", [128, D])
            P.dma(gfin[:], W["norm_final_g"], writes=[gfin])
            h2 = sb(st, "h2f", [128, 16, TB + 2], BF16, n=1)
            wup = sb(st, "wup", [128, 16, 512], BF16, n=2)
            wdn = sb(st, "wdn", [128, 22, 512], BF16, n=2)
            U = sb(st, "U", [128, TB + 2], F32, n=3)
            cvg = sb(st, "cvg", [128, TB], F32, n=2)
            cvv = sb(st, "cvv", [128, TB], F32, n=2)
            ctmp5 = sb(st, "ctmp5", [128, TB], F32)
            actT = sb(st, "actT", [128, 44, TB], BF16, n=1)
            x1t = [sb(st, f"x1t{i}", [128, D], F32, n=1) for i in range(4)]
            junk = sb(st, "junk5", [128, D], BF16, n=1)
            ss = sb(st, "ss5", [128, 1], F32, n=2)
            rstd = sb(st, "rstd5", [128, 1], F32, n=2)
            pF = ps(st, "pF", [128, 512], F32, n=2)
            pTl_t = st.enter_context(nc.psum_tensor("pTl", [128, 512], F32))
            pTl = [Buf(pTl_t, f"ptl{k}") for k in range(8)]
            pD = ps(st, "pD", [128, 512], F32, n=4)
            H2v = H2T.rearrange("(kc p) t -> p kc t", p=128)
            tl_i = 0
            for b in range(NB):
                tok0 = b * TB
                lo = max(tok0 - 1, 0); hi = min(tok0 + TB + 1, TT)
                if lo > tok0 - 1:
                    P.op("pool", lambda e: e.memset(h2[:, :, 0:1], 0.0), writes=[h2])
                if hi < tok0 + TB + 1:
                    P.op("pool", lambda e: e.memset(h2[:, :, TB + 1:TB + 2], 0.0), writes=[h2])
                deps = [H2T_B[bb] for bb in range(lo // TB, (hi - 1) // TB + 1)]
                P.dma(h2[:, :, lo - (tok0 - 1):hi - (tok0 - 1)], H2v[:, :, lo:hi], reads=deps, writes=[h2], part=True)
                if tok0 % SL == 0 and tok0 > 0:
                    s = tok0 // SL
                    P.op("pool", lambda e: e.tensor_scalar(h2[:, :, 0:1], h2[:, :, 0:1], carry_t[:, s:s + 1], None, ALU.mult), reads=[carry_t], writes=[h2])
                if (tok0 + TB) % SL == 0 and tok0 + TB < TT:
                    s = (tok0 + TB) // SL
                    P.op("pool", lambda e: e.tensor_scalar(h2[:, :, TB + 1:TB + 2], h2[:, :, TB + 1:TB + 2], carry_t[:, s:s + 1], None, ALU.mult),
                         reads=[carry_t], writes=[h2])
                wcache = {}
                for j in range(44):
                    res = []
                    for half in range(2):
                        ch = half * 44 + j
                        g, jj = ch // 4, ch % 4
                        if g not in wcache:
                            w_ = wup.next()
                            P.dma(w_[:], WS["ffn_up"]["ap"][g], reads=[WS["ffn_up"]["bufs"][g]], writes=[w_])
                            for k_ in [k_ for k_ in wcache if (k_ < 11) == (g < 11)]:
                                del wcache[k_]
                            wcache[g] = w_
                        w_ = wcache[g]
                        p_ = pF.next()
                        for kc in range(16):
                            P.op("pe", lambda e: e.matmul(p_[:], w_[:, kc, jj * 128:(jj + 1) * 128], h2[:, kc, 0:512], start=(kc == 0), stop=(kc == 15)),
                                 reads=[w_, h2], writes=[p_], inc=(kc == 15))
                        tl = pTl[tl_i % 8]; tc = (tl_i % 8) * 64; tl_i += 1
                        for kc in range(16):
                            P.op("pe", lambda e: e.matmul(pTl_t[:, tc:tc + 2], w_[:, kc, jj * 128:(jj + 1) * 128], h2[:, kc, 512:514], start=(kc == 0), stop=(kc == 15)),
                                 reads=[w_, h2], writes=[tl], inc=(kc == 15))
                        u = U.next()
                        P.op("act", lambda e: e.copy(u[:, 0:512], p_[:]), reads=[p_], writes=[u])
                        P.op("dve", lambda e: e.tensor_copy(u[:, 512:514], pTl_t[:, tc:tc + 2]), reads=[tl], writes=[u])
                        cv = (cvg if half == 0 else cvv).next()
                        eng = "dve" if half == 0 else "pool"
                        P.op(eng, lambda e: e.tensor_scalar(cv[:], u[:, 0:TB], fcw[:, ch, 0:1], fcb[:, ch:ch + 1], ALU.mult, ALU.add),
                             reads=[u, fcw, fcb], writes=[cv])
                        for t_ in (1, 2):
                            if eng == "dve":
                                P.op(eng, lambda e: e.scalar_tensor_tensor(cv[:], u[:, t_:t_ + TB], fcw[:, ch, t_:t_ + 1], cv[:], ALU.mult, ALU.add),
                                     reads=[u, fcw], writes=[cv])
                            else:
                                P.op(eng, lambda e: e.tensor_scalar(ctmp5[:], u[:, t_:t_ + TB], fcw[:, ch, t_:t_ + 1], None, ALU.mult),
                                     reads=[u, fcw], writes=[ctmp5])
                                P.op(eng, lambda e: e.tensor_tensor(cv[:], cv[:], ctmp5[:], ALU.add), reads=[ctmp5], writes=[cv])
                        res.append(cv)
                    P.op("act", lambda e: e.activation(res[0][:], res[0][:], AF.Silu), writes=[res[0]])
                    P.op("dve", lambda e: e.tensor_tensor(actT[:, j, :], res[0][:], res[1][:], ALU.mult), reads=[res[0], res[1]], writes=[actT])
                for i in range(4):
                    t0 = tok0 + i * 128
                    P.dma(x1t[i][:], X1[t0:t0 + 128, :], reads=[X1_B[t0 // 128]], writes=[x1t[i]])
                for n in range(4):
                    pds = [pD.next() for i in range(4)]
                    for half in range(2):
                        w_ = wdn.next()
                        P.dma(w_[:], WS["ffn_down"]["ap"][n, :, half * 22:(half + 1) * 22, :], reads=[WS["ffn_down"]["bufs"][n]], writes=[w_])
                        for i in range(4):
                            for kc in range(22):
                                P.op("pe", lambda e: e.matmul(pds[i][:], actT[:, half * 22 + kc, i * 128:(i + 1) * 128], w_[:, kc, :],
                                                              start=(half == 0 and kc == 0), stop=(half == 1 and kc == 21)),
                                     reads=[actT, w_], writes=[pds[i]], inc=(kc == 21))
                    for i in range(4):
                        P.op("dve", lambda e: e.tensor_tensor(x1t[i][:, n * 512:(n + 1) * 512], pds[i][:], x1t[i][:, n * 512:(n + 1) * 512], ALU.add),
                             reads=[pds[i]], writes=[x1t[i]])
                for i in range(4):
                    t0 = tok0 + i * 128
                    s_ = ss.next(); r_ = rstd.next()
                    rms_rstd(x1t[i], s_, r_, junk)
                    P.op("act", lambda e: e.activation(x1t[i][:], x1t[i][:], AF.Copy, scale=r_[:]), reads=[r_], writes=[x1t[i]])
                    P.op("pool", lambda e: e.tensor_tensor(x1t[i][:], x1t[i][:], gfin[:], ALU.mult), reads=[gfin], writes=[x1t[i]])
                    P.dma(y_out[t0:t0 + 128, :], x1t[i][:], reads=[x1t[i]], owner=x1t[i])
            P.barrier()
            P.release(dummy)

    if upto >= 5:
        stage5()

    P.finish()
    return nc, es


def _consts():
    s = np.arange(128)[:, None]
    t = np.arange(128)[None, :]
    same = (s // 64) == (t // 64)
    tri = np.zeros((2, 128, 384), np.float32)
    mask = np.zeros((2, 128, 384), np.float32)
    tri[0, :, 0:128] = CDEC * (same & (s <= t))
    tri[0, :, 128:256] = CDEC * (same & (s < t))
    tri[0, :, 256:384] = CDEC * same
    tri[1, :, 0:128] = CDEC * (same & (s >= t))
    tri[1, :, 128:256] = CDEC * (same & (s > t))
    tri[1, :, 256:384] = CDEC * same
    mask[0, :, 0:128] = same & (s < t)
    mask[0, :, 128:256] = same & (s <= t)
    mask[0, :, 256:384] = same & (s > t)
    mask[1, :, 0:128] = same & (s > t)
    mask[1, :, 128:256] = same & (s >= t)
    mask[1, :, 256:384] = same & (s < t)
    ident = np.eye(128, dtype=np.float32)
    bones = ((s // 64) == (t // 64)).astype(np.float32)
    return tri, mask, ident, bones


def _invcnt(SL, NSLOT, carry):
    TT = SL * NSLOT
    seq_id = np.zeros(TT, np.int64)
    cur = 0
    for s in range(NSLOT):
        if s > 0 and not carry[s]:
            cur += 1
        seq_id[s * SL:(s + 1) * SL] = cur
    out = np.zeros((4, TT), np.float32)
    for sid in np.unique(seq_id):
        idx = np.nonzero(seq_id == sid)[0]
        T = len(idx)
        t = np.arange(T)
        for g, half in enumerate(POOL_HALF):
            lo = np.clip(t - half, 0, T)
            hi = np.clip(t + half, 0, T)
            out[g, idx] = 1.0 / (hi - lo)
    return out


def _pc(v, n):
    return np.ascontiguousarray(np.asarray(v, np.float32).reshape(n, 128).T)


def prep_shared(inp):
    tri, mask, ident, bones = _consts()
    g = lambda k: np.asarray(inp[k], np.float32)
    sh = {
        "w_in": g("w_in")[0], "p_pool": g("p_pool")[0], "p_rwkv": g("p_rwkv")[0], "w_out": g("w_out")[0],
        "ffn_up": g("ffn_up")[0], "ffn_down": g("ffn_down")[0], "pool_w": g("pool_w")[0],
        "w_up": g("w_up")[0].reshape(128, DR), "a_up": g("a_up")[0].reshape(128, DR), "g_up": g("g_up")[0],
        "norm_mix_g": _pc(g("norm_mix_g")[0], 16), "norm_ffn_g": _pc(g("norm_ffn_g")[0], 16),
        "norm_final_g": np.broadcast_to(g("norm_final_g").reshape(1, D), (128, D)),
        "shift_w": np.ascontiguousarray(g("shift_w")[0].reshape(3, 27, 128).transpose(2, 1, 0)),
        "pool_scale": _pc(g("pool_scale")[0], 8), "w0": np.broadcast_to(g("w0")[0][:, None, :], (2, 128, DR)),
        "a0": np.ascontiguousarray(g("a0")[0].reshape(2, 8, 128).transpose(2, 0, 1)),
        "k_k": _pc(g("k_k")[0], 8), "k_a": _pc(g("k_a")[0], 8),
        "r_k": np.ascontiguousarray(g("r_k")[0].reshape(2, 8, 128).transpose(2, 0, 1)),
        "lnx_w": _pc(g("lnx_w")[0], 8), "lnx_b": _pc(g("lnx_b")[0], 8),
        "ffn_conv_w": np.ascontiguousarray(g("ffn_conv_w")[0].reshape(3, 88, 128).transpose(2, 1, 0)),
        "ffn_conv_b": _pc(g("ffn_conv_b")[0], 88),
        "c_tri": tri, "c_mask": mask, "c_ident": ident, "c_bones": bones,
    }
    cm = np.zeros((128, CM_N), np.float32)
    for nm, arr in (("tri0", tri[0]), ("tri1", tri[1]), ("msk0", mask[0]), ("msk1", mask[1]), ("ident", ident), ("bones", bones)):
        o, shp = CM_OFF[nm]
        cm[:, o:o + int(np.prod(shp))] = arr
    cv = np.zeros((128, CV_N), np.float32)
    for nm, (o, shp) in CV_OFF.items():
        if nm == "carry":
            continue
        cv[:, o:o + int(np.prod(shp))] = np.asarray(sh[nm], np.float32).reshape(128, -1)
    out = {k: np.ascontiguousarray(sh[k], dtype=np.float32) for k in
           ("w_in", "p_pool", "p_rwkv", "w_out", "ffn_up", "ffn_down", "pool_w", "w_up", "a_up", "g_up", "norm_final_g", "w0")}
    out["cmat"] = cm
    out["cvec"] = cv
    return out


def core_maps(shared, xs_list, carries, SL, NSLOT):
    maps = []
    for xc, cr in zip(xs_list, carries):
        m = dict(shared)
        m["x"] = np.ascontiguousarray(xc, dtype=np.float32)
        m["invcnt"] = np.ascontiguousarray(np.broadcast_to(_invcnt(SL, NSLOT, cr)[:, None, :], (4, 128, SL * NSLOT)))
        cv = m["cvec"].copy()
        o, _ = CV_OFF["carry"]
        for s in range(1, NSLOT):
            cv[:, o + s] = float(cr[s])
        m["cvec"] = cv
        maps.append(m)
    return maps


def kernel(**inp):
    SL, NSLOT = 2048, 4
    xp = np.asarray(inp["x_prompt"], np.float32)
    xsm = np.asarray(inp["x_sample"], np.float32)
    assign = [[0, 1, 2], [3, 4, 5], [6, 7, 8], [9, 10, 11], [12, 13], [14, 15]]
    xs_list, carries = [], []
    for b in range(2):
        xs_list.append(xp[b])
        carries.append([0, 1, 1, 1])
    for a in assign:
        xc = np.zeros((SL * NSLOT, D), np.float32)
        for s, q in enumerate(a):
            xc[s * SL:(s + 1) * SL] = xsm[q]
        xs_list.append(xc)
        carries.append([0, 0, 0, 0])
    nc, es = build(SL, NSLOT)
    maps = core_maps(prep_shared(inp), xs_list, carries, SL, NSLOT)
    with es:
        res = run_bass_kernel_spmd(nc, maps, core_ids=list(range(NCORES)))
    yp = np.stack([res.results[b]["y"] for b in range(2)], 0)
    ys = np.zeros_like(xsm)
    for ci, a in enumerate(assign):
        yc = res.results[2 + ci]["y"]
        for s, q in enumerate(a):
            ys[q] = yc[s * SL:(s + 1) * SL]
    return (yp.astype(np.float32), ys.astype(np.float32))
```

```python
import numpy as np
from contextlib import ExitStack
import concourse.bass as bass
import concourse.mybir as mybir
from concourse.bass_utils import run_bass_kernel_spmd

F32 = mybir.dt.float32
BF16 = mybir.dt.bfloat16
ALU = mybir.AluOpType
AF = mybir.ActivationFunctionType
AX = mybir.AxisListType

D = 2048
DP = 1024
DR = 1024
DRIN = 3456
DIN = 8576
DFF = 5632
NCORES = 8
SCAN_CUT = 9
EPS = 1e-6
LNX_EPS = 64e-5
CDEC = -float(np.exp(-0.5))
POOL_HALF = (1, 2, 4, 8)


CV_LAYOUT = [("carry", [8]), ("norm_mix_g", [16]), ("norm_ffn_g", [16]), ("shift_w", [27, 3]), ("pool_scale", [8]),
             ("a0", [2, 8]), ("k_k", [8]), ("k_a", [8]), ("r_k", [2, 8]), ("lnx_w", [8]), ("lnx_b", [8]),
             ("ffn_conv_w", [88, 3]), ("ffn_conv_b", [88])]
CM_LAYOUT = [("tri0", [384]), ("tri1", [384]), ("msk0", [384]), ("msk1", [384]), ("ident", [128]), ("bones", [128])]


def _offsets(layout):
    off, o = {}, 0
    for nm, shp in layout:
        n = int(np.prod(shp))
        off[nm] = (o, shp)
        o += (n + 15) // 16 * 16
    return off, o


CV_OFF, CV_N = _offsets(CV_LAYOUT)
CM_OFF, CM_N = _offsets(CM_LAYOUT)


class Buf:
    __slots__ = ("t", "w", "r", "dsem", "dcnt", "name", "keep")

    def __init__(self, t=None, name=""):
        self.t = t
        self.w = {}
        self.r = {}
        self.dsem = None
        self.dcnt = 0
        self.name = name
        self.keep = False

    def __getitem__(self, k):
        return self.t[k]


class View(Buf):
    def __init__(self, parent, ap):
        self.t = ap
        self.w = parent.w
        self.r = parent.r
        self.dsem = None
        self.dcnt = 0
        self.name = "view"
        self.keep = True


class Sem:
    __slots__ = ("h",)

    def __init__(self, h):
        self.h = h


class Prog:
    EPOCH = 30000

    def __init__(self, nc, es):
        self.nc = nc
        self.es = es
        self.eng = {"pe": nc.tensor, "act": nc.scalar, "dve": nc.vector, "pool": nc.gpsimd, "sp": nc.sync}
        self.sem = {}
        self.cnt = {}
        self.seen = {e: {} for e in self.eng}
        self.nsem = 0
        self.allsems = []
        self.dead = set()
        self.recycled = []
        self._init_sems()
        for e in self.eng:
            self._new_epoch(e)
        self.dsems = []

    def _init_sems(self):
        self.free = []
        while True:
            try:
                self.free.append(self.nc.alloc_semaphore(name=f"sem{len(self.free)}"))
            except Exception:
                break
        for h in self.free:
            self.nc.gpsimd.sem_clear(h)
        self.nc.all_engine_barrier()

    def _newsem(self, name):
        self.nsem += 1
        h = Sem(self.free.pop())
        self.allsems.append(h)
        return h

    def _new_epoch(self, e):
        self.sem[e] = self._newsem("e" + e)
        self.cnt[e] = 0

    def release(self, dummy):
        rel = [b for b in self.dsems if not b.keep]
        import os as _os
        if "keepall" in _os.environ.get("LWV", ""):
            rel = []
        if not rel:
            return
        self.nc.all_engine_barrier()
        for b in rel:
            self.nc.gpsimd.sem_clear(b.dsem.h)
            self.dead.add(id(b.dsem))
            self.free.append(b.dsem.h)
            b.dsem = None
            b.dcnt = 0
        self.dsems = [b for b in self.dsems if b.keep]
        self.nc.all_engine_barrier()

    def _wait(self, e, deps):
        need = {}
        for s, c in deps:
            k = id(s)
            if k in self.dead:
                continue
            if k not in need or need[k][1] < c:
                need[k] = (s, c)
        for k, (s, c) in need.items():
            if e == "pe" and s is self.sem["pe"]:
                continue
            if self.seen[e].get(k, 0) >= c:
                continue
            self.eng[e].wait_ge(s.h, c)
            self.seen[e][k] = c

    @staticmethod
    def _add(d, tok):
        k = id(tok[0])
        if k not in d or d[k][1] < tok[1]:
            d[k] = tok

    def op(self, e, fn, reads=(), writes=(), inc=True):
        deps = []
        for b in reads:
            deps.extend(b.w.values())
        for b in writes:
            deps.extend(b.w.values())
            deps.extend(b.r.values())
        self._wait(e, deps)
        ins = fn(self.eng[e])
        if inc:
            self.cnt[e] += 1
            ins.then_inc(self.sem[e].h, 1)
            tok = (self.sem[e], self.cnt[e])
        else:
            tok = (self.sem[e], self.cnt[e] + 1)
        for b in reads:
            self._add(b.r, tok)
        for b in writes:
            b.w = {id(tok[0]): tok}
            b.r = {}
        if inc and self.cnt[e] >= self.EPOCH:
            self._new_epoch(e)
        return ins

    def dma(self, out_ap, in_ap, reads=(), writes=(), q="sp", owner=None, part=False):
        if owner is None:
            owner = writes[0] if writes and writes[0].t is not None else reads[0]
        if owner.dsem is None:
            owner.dsem = self._newsem("d")
            self.dsems.append(owner)
        deps = []
        for b in reads:
            deps.extend(b.w.values())
        for b in writes:
            for tok in b.w.values():
                if part and tok[0] is owner.dsem:
                    continue
                deps.append(tok)
            deps.extend(b.r.values())
        self._wait(q, deps)
        owner.dcnt += 16
        self.eng[q].dma_start(out=out_ap, in_=in_ap).then_inc(owner.dsem.h, 16)
        tok = (owner.dsem, owner.dcnt)
        import os as _os
        if "serial" in _os.environ.get("LWV", ""):
            self._wait(q, [tok])
        for b in reads:
            self._add(b.r, tok)
        for b in writes:
            if part:
                self._add(b.w, tok)
            else:
                b.w = {id(tok[0]): tok}
                b.r = {}

    def barrier(self):
        toks = [(self.sem[e], self.cnt[e]) for e in self.eng if self.cnt[e] > 0]
        toks += [(b.dsem, b.dcnt) for b in self.dsems if b.dcnt > 0]
        for e in self.eng:
            self._wait(e, toks)

    def finish(self):
        deps = [(b.dsem, b.dcnt) for b in self.dsems if b.dcnt > 0]
        self._wait("sp", deps)


class Ring:
    def __init__(self, bufs):
        self.bufs = bufs
        self.i = 0

    def next(self):
        b = self.bufs[self.i % len(self.bufs)]
        self.i += 1
        return b


def build(SL=2048, NSLOT=4, debug=(), upto=9, skip_front=False):
    TT = SL * NSLOT
    TB = 512
    NB = TT // TB
    BPS = SL // TB
    nc = bass.Bass("TRN2", target_bir_lowering=False)
    es = ExitStack()
    P = Prog(nc, es)
    gstack = ExitStack()
    es.enter_context(gstack)

    def din(name, shape, dt=F32):
        return nc.dram_tensor(name, list(shape), dt, kind="ExternalInput").ap()

    def dscr(name, shape, dt=F32):
        kind = "ExternalOutput" if name in debug else "Internal"
        return nc.dram_tensor(name, list(shape), dt, kind=kind).ap()

    uid = [0]

    def sb(stack, name, shape, dt=F32, n=1, ring=False):
        uid[0] += 1
        name = f"{name}u{uid[0]}_"
        nbytes = int(np.prod(shape[1:])) * (2 if dt == BF16 else 4)
        bufs = []
        for i in range(n):
            bufs.append(Buf(stack.enter_context(nc.sbuf_tensor(f"{name}{i}", list(shape), dt)), f"{name}{i}"))
            if nbytes % 64 != 0:
                padb = 64 - (((nbytes + 31) // 32 * 32) % 64)
                if padb != 64:
                    stack.enter_context(nc.sbuf_tensor(f"{name}{i}pad", [128, padb // 4], F32))
        for b_ in bufs:
            b_.keep = stack is gstack
        return Ring(bufs) if (n > 1 or ring) else bufs[0]

    def ps(stack, name, shape, dt=F32, n=1):
        uid[0] += 1
        name = f"{name}u{uid[0]}_"
        bufs = [Buf(stack.enter_context(nc.psum_tensor(f"{name}{i}", list(shape), dt)), f"{name}{i}") for i in range(n)]
        return bufs[0] if n == 1 else Ring(bufs)

    x_in = din("x", [TT, D])
    y_out = nc.dram_tensor("y", [TT, D], F32, kind="ExternalOutput").ap()
    class LazyIn(dict):
        def __init__(self, shapes):
            super().__init__()
            self.shapes = dict(shapes)

        def __missing__(self, nm):
            self[nm] = din(nm, self.shapes[nm])
            return self[nm]

    W = LazyIn([("w_in", [D, DIN]), ("p_pool", [DP, D]), ("p_rwkv", [DR, D]), ("w_out", [D, D]),
                    ("ffn_up", [D, 2 * DFF]), ("ffn_down", [DFF, D]), ("pool_w", [4, 256, 256]),
                    ("w_up", [128, DR]), ("a_up", [128, DR]), ("g_up", [128, DR]),
                    ("norm_mix_g", [128, 16]), ("norm_ffn_g", [128, 16]), ("norm_final_g", [128, D]),
                    ("shift_w", [128, 27, 3]), ("pool_scale", [128, 8]), ("w0", [2, 128, DR]), ("a0", [128, 2, 8]),
                    ("k_k", [128, 8]), ("k_a", [128, 8]), ("r_k", [128, 2, 8]), ("lnx_w", [128, 8]), ("lnx_b", [128, 8]),
                    ("ffn_conv_w", [128, 88, 3]), ("ffn_conv_b", [128, 88]),
                    ("invcnt", [4, 128, TT]), ("carry", [128, NSLOT + 1]),
                    ("cvec", [128, CV_N]), ("cmat", [128, CM_N])])

    def wscr(name, K, N):
        KC = K // 128
        NG = (N + 511) // 512
        return dict(ap=dscr("s_" + name, [NG, 128, KC, 512], BF16), KC=KC, NG=NG, N=N,
                    bufs=[Buf(None, f"{name}_g{g}") for g in range(NG)])

    WS = {nm: wscr(nm, k, n) for nm, k, n in [("w_in", D, DIN), ("p_pool", DP, D), ("p_rwkv", DR, D), ("w_out", D, D),
                                                ("ffn_up", D, 2 * DFF), ("ffn_down", DFF, D)]}
    if skip_front:
        PROJ_A = din("s_proj", [35 * 128, TT])
        PROJ_G = din("s_projg", [32 * 128, TT])
    else:
        PROJ_A = dscr("s_proj", [35 * 128, TT])
        PROJ_G = dscr("s_projg", [32 * 128, TT])

    def prow(j):
        return PROJ_A[j * 128:(j + 1) * 128, :] if j < 35 else PROJ_G[(j - 35) * 128:(j - 34) * 128, :]
    PROJ_B = [[Buf(None, f"proj{j}_{b}") for b in range(NB)] for j in range(DIN // 128)]

    cvec_t = sb(gstack, "cvec", [128, CV_N])
    cmat_t = sb(gstack, "cmat", [128, CM_N])
    P.dma(cvec_t[:], W["cvec"], writes=[cvec_t])
    P.dma(cmat_t[:], W["cmat"], writes=[cmat_t])

    def cview(tile, offs, nm):
        o, shp = offs[nm]
        n = int(np.prod(shp))
        ap = tile[:, o:o + n]
        if len(shp) == 2:
            ap = ap.rearrange("p (a b) -> p a b", b=shp[1])
        return View(tile, ap)

    CVW = {nm: cview(cvec_t, CV_OFF, nm) for nm in CV_OFF}
    CMW = {nm: cview(cmat_t, CM_OFF, nm) for nm in CM_OFF}
    ident_f = CMW["ident"]
    gmix = CVW["norm_mix_g"]
    ident_b = sb(gstack, "identb", [128, 128], BF16)
    eps_t = sb(gstack, "eps", [128, 16])
    dummy = sb(gstack, "dummy", [128, 16])
    P.op("dve", lambda e: e.tensor_copy(ident_b[:], ident_f[:]), reads=[ident_f], writes=[ident_b])
    P.op("dve", lambda e: e.memset(eps_t[:], EPS), writes=[eps_t])

    import os as _os0
    if "early" in _os0.environ.get("LWV", ""):
        e_f = sb(gstack, "earlyf", [128, DR])
        e_b = sb(gstack, "earlyb", [128, DR], BF16)
        P.dma(e_f[:], W["w_up"], writes=[e_f])
        P.op("dve", lambda e: e.tensor_copy(e_b[:], e_f[:]), reads=[e_f], writes=[e_b])

    def convert_weights(names):
        with ExitStack() as st:
            stg = sb(st, "cvf", [128, 8, 512], F32, n=2)
            stb = sb(st, "cvb", [128, 8, 512], BF16, n=2)
            k = 0
            for nm in names:
                ws = WS[nm]
                src = W[nm].rearrange("(kc p) n -> p kc n", p=128)
                for g in range(ws["NG"]):
                    ncol = min(512, ws["N"] - g * 512)
                    for k0 in range(0, ws["KC"], 8):
                        kn = min(8, ws["KC"] - k0)
                        f = stg.next()
                        b = stb.next()
                        P.dma(f[:, 0:kn, 0:ncol], src[:, k0:k0 + kn, g * 512:g * 512 + ncol], writes=[f])
                        e = ("pool", "dve", "act")[k % 3]
                        k += 1
                        if e == "act":
                            P.op(e, lambda en: en.copy(b[:, 0:kn, 0:ncol], f[:, 0:kn, 0:ncol]), reads=[f], writes=[b])
                        else:
                            P.op(e, lambda en: en.tensor_copy(b[:, 0:kn, 0:ncol], f[:, 0:kn, 0:ncol]), reads=[f], writes=[b])
                        P.dma(ws["ap"][g, :, k0:k0 + kn, 0:ncol], b[:, 0:kn, 0:ncol], reads=[b], writes=[ws["bufs"][g]],
                              owner=b, part=True)
            P.barrier()
            P.release(dummy)

    if not skip_front:
        convert_weights(["w_in"])

    def stage1():
        with ExitStack() as st:
            xt = sb(st, "xt", [128, D], F32, n=2)
            junk = sb(st, "junk", [128, D], BF16, n=1)
            xs = sb(st, "xs", [128, D], BF16, n=2)
            ss = sb(st, "ss", [128, 1], F32, n=2)
            rstd = sb(st, "rstd", [128, 1], F32, n=2)
            hT = sb(st, "hT", [128, 16, TB], BF16, n=4)
            wt = sb(st, "wt", [128, 16, 512], BF16, n=3)
            og = sb(st, "og", [128, TB], F32, n=4)
            ptr = ps(st, "ptr", [128, 16, 128], BF16, n=1)
            pg = ps(st, "pg", [128, TB], F32, n=4)
            ws = WS["w_in"]
            k = 0
            GB = 2
            for b0 in range(0, NB, GB):
                hb_ = []
                for b in range(b0, min(b0 + GB, NB)):
                    h = hT.next()
                    hb_.append((b, h))
                    for i in range(TB // 128):
                        t0 = b * TB + i * 128
                        x = xt.next()
                        P.dma(x[:], x_in[t0:t0 + 128, :], writes=[x])
                        s_ = ss.next()
                        r_ = rstd.next()
                        xs_ = xs.next()
                        P.op("act", lambda e: e.activation(junk[:], x[:], AF.Square, accum_out=s_[:]), reads=[x], writes=[junk, s_])
                        P.op("dve", lambda e: e.tensor_scalar(r_[:], s_[:], 1.0 / D, EPS, ALU.mult, ALU.add), reads=[s_], writes=[r_])
                        P.op("act", lambda e: e.sqrt(r_[:], r_[:]), reads=[r_], writes=[r_])
                        P.op("dve", lambda e: e.reciprocal(r_[:], r_[:]), reads=[r_], writes=[r_])
                        P.op("act", lambda e: e.activation(xs_[:], x[:], AF.Copy, scale=r_[:]), reads=[x, r_], writes=[xs_])
                        for kc in range(16):
                            P.op("pe", lambda e: e.transpose(ptr[:, kc, :], xs_[:, kc * 128:(kc + 1) * 128], ident_b[:]),
                                 reads=[xs_, ident_b], writes=[ptr], inc=(kc == 15))
                        P.op("dve", lambda e: e.tensor_tensor(h[:, :, i * 128:(i + 1) * 128], ptr[:],
                                                              gmix[:].unsqueeze(2).to_broadcast([128, 16, 128]), ALU.mult),
                             reads=[ptr, gmix], writes=[h])
                for g in range(ws["NG"]):
                    w_ = wt.next()
                    ncol = min(512, ws["N"] - g * 512)
                    P.dma(w_[:, :, 0:ncol], ws["ap"][g, :, :, 0:ncol], reads=[ws["bufs"][g]], writes=[w_])
                    for b, h in hb_:
                        for jj in range(ncol // 128):
                            j = g * 4 + jj
                            pt = pg.next()
                            for kc in range(16):
                                P.op("pe", lambda e: e.matmul(pt[:], w_[:, kc, jj * 128:(jj + 1) * 128], h[:, kc, :],
                                                              start=(kc == 0), stop=(kc == 15)),
                                     reads=[w_, h], writes=[pt], inc=(kc == 15))
                            o = og.next()
                            gate = j >= (DP + DRIN) // 128
                            if gate:
                                P.op("act", lambda e: e.activation(o[:], pt[:], AF.Sigmoid), reads=[pt], writes=[o])
                            elif k % 2 == 0:
                                P.op("dve", lambda e: e.tensor_copy(o[:], pt[:]), reads=[pt], writes=[o])
                            else:
                                P.op("act", lambda e: e.copy(o[:], pt[:]), reads=[pt], writes=[o])
                            k += 1
                            P.dma(prow(j)[:, b * TB:(b + 1) * TB], o[:], reads=[o], writes=[PROJ_B[j][b]], owner=o)
            P.barrier()
            P.release(dummy)

    if not skip_front:
        stage1()

    carry_t = CVW["carry"]

    def load_halo(dst, rows_ap, bufs_row, tok0, n, hl):
        lo = max(tok0 - hl, 0)
        hi = min(tok0 + n + hl, TT)
        if lo > tok0 - hl:
            P.op("pool", lambda e: e.memset(dst[:, 0:hl], 0.0), writes=[dst])
        if hi < tok0 + n + hl:
            P.op("pool", lambda e: e.memset(dst[:, hl + n:hl + n + hl], 0.0), writes=[dst])
        deps = [bufs_row[bb] for bb in range(lo // TB, (hi - 1) // TB + 1)]
        P.dma(dst[:, lo - (tok0 - hl):hi - (tok0 - hl)], rows_ap[:, lo:hi], reads=deps, writes=[dst], part=True)
        if tok0 % SL == 0 and tok0 > 0:
            s = tok0 // SL
            P.op("pool", lambda e: e.tensor_scalar(dst[:, 0:hl], dst[:, 0:hl], carry_t[:, s:s + 1], None, ALU.mult),
                 reads=[carry_t], writes=[dst])
        if (tok0 + n) % SL == 0 and tok0 + n < TT:
            s = (tok0 + n) // SL
            P.op("pool", lambda e: e.tensor_scalar(dst[:, hl + n:hl + n + hl], dst[:, hl + n:hl + n + hl],
                                                   carry_t[:, s:s + 1], None, ALU.mult), reads=[carry_t], writes=[dst])

    def conv3(e, out, src, wt3, c, n, reads, tmp=None):
        P.op(e, lambda en: en.tensor_scalar(out[:, 0:n], src[:, 0:n], wt3[:, c, 0:1], None, ALU.mult),
             reads=[src, wt3] + reads, writes=[out])
        for j in (1, 2):
            if e == "dve":
                P.op(e, lambda en: en.scalar_tensor_tensor(out[:, 0:n], src[:, j:j + n], wt3[:, c, j:j + 1], out[:, 0:n],
                                                           ALU.mult, ALU.add), reads=[src, wt3], writes=[out])
            else:
                P.op(e, lambda en: en.tensor_scalar(tmp[:, 0:n], src[:, j:j + n], wt3[:, c, j:j + 1], None, ALU.mult),
                     reads=[src, wt3], writes=[tmp])
                P.op(e, lambda en: en.tensor_tensor(out[:, 0:n], out[:, 0:n], tmp[:, 0:n], ALU.add), reads=[tmp], writes=[out])

    YF = dscr("s_yf", [TT, DR])
    YF_B = [[Buf(None, f"yf{b}_{hp}") for hp in range(8)] for b in range(NB)]
    RY = dscr("s_ry", [DR, TT], BF16)
    RY_B = [[Buf(None, f"ry{hp}_{b}") for b in range(NB)] for hp in range(8)]

    def scan_pass(dr):
        od = 1 - dr
        with ExitStack() as st:
            tri = CMW[f"tri{dr}"]
            msk = CMW[f"msk{dr}"]
            bones = CMW["bones"]
            shw = CVW["shift_w"]
            vec = {nm: CVW[nm] for nm in ("k_k", "k_a", "lnx_w", "lnx_b", "a0", "r_k")}
            if SCAN_CUT == 0:
                P.barrier()
                P.release(dummy)
                return
            omka = sb(st, "omka", [128, 8])
            import os as _os3
            if "noomka" not in _os3.environ.get("LWV", ""):
                P.op("dve", lambda e: e.tensor_scalar(omka[:], vec["k_a"][:], -1.0, 1.0, ALU.mult, ALU.add), reads=[vec["k_a"]], writes=[omka])
            if SCAN_CUT == -2:
                P.barrier()
                P.release(dummy)
                return
            w0b = sb(st, "w0b", [128, DR])
            import os as _os2
            if "now0" not in _os2.environ.get("LWV", ""):
                P.dma(w0b[:], W["w0"][dr], writes=[w0b])
            if SCAN_CUT == -3:
                P.barrier()
                P.release(dummy)
                return
            lw = {nm: sb(st, "lw_" + nm, [128, DR], BF16) for nm in ("w_up", "a_up", "g_up")}
            with ExitStack() as st2:
                tmpf = sb(st2, "lwf", [128, DR])
                import os as _os
                _v = _os.environ.get("LWV", "")
                for nm in ("w_up", "a_up", "g_up")[:(1 if "one" in _v else 3)]:
                    if "v9" in _v:
                        P.dma(tmpf[:, 0:128], W["c_ident"], writes=[tmpf])
                        P.op("dve", lambda e: e.tensor_copy(lw[nm][:, 0:128], tmpf[:, 0:128]), reads=[tmpf], writes=[lw[nm]])
                        continue
                    if "v6" in _v:
                        P.dma(tmpf[:, 0:384], W["c_tri"][1], writes=[tmpf])
                        P.op("dve", lambda e: e.tensor_copy(lw[nm][:, 0:384], tmpf[:, 0:384]), reads=[tmpf], writes=[lw[nm]])
                        continue
                    if "v5" in _v:
                        P.dma(tmpf[:, 0:512], W[nm][:, 0:512], writes=[tmpf], part=True)
                        P.dma(tmpf[:, 512:1024], W[nm][:, 512:1024], writes=[tmpf], part=True)
                    else:
                        P.dma(tmpf[:], W[nm], writes=[tmpf])
                    if "nocast" not in _v:
                        if "v3" in _v:
                            P.op("dve", lambda e: e.memset(lw["a_up"][:], 1.0), writes=[lw["a_up"]])
                            P.op("dve", lambda e: e.tensor_copy(lw[nm][:], lw["a_up"][:]), reads=[lw["a_up"]], writes=[lw[nm]])
                        elif "v4" in _v:
                            P.op("dve", lambda e: e.tensor_copy(omka[:], w0b[:, 0:8]), reads=[w0b], writes=[omka])
                        elif "v1" in _v:
                            P.op("dve", lambda e: e.tensor_copy(lw[nm][:], w0b[:]), reads=[w0b], writes=[lw[nm]])
                        elif "v2" in _v:
                            P.op("dve", lambda e: e.tensor_copy(w0b[:, 0:512], tmpf[:, 0:512]), reads=[tmpf], writes=[w0b])
                        elif "act" in _v:
                            P.op("act", lambda e: e.copy(lw[nm][:], tmpf[:]), reads=[tmpf], writes=[lw[nm]])
                        elif "pool" in _v:
                            P.op("pool", lambda e: e.tensor_copy(lw[nm][:], tmpf[:]), reads=[tmpf], writes=[lw[nm]])
                        elif "half" in _v:
                            P.op("dve", lambda e: e.tensor_copy(lw[nm][:, 0:512], tmpf[:, 0:512]), reads=[tmpf], writes=[lw[nm]])
                        else:
                            P.op("dve", lambda e: e.tensor_copy(lw[nm][:], tmpf[:]), reads=[tmpf], writes=[lw[nm]])
                P.barrier()
            if "norel" not in _v:
                P.release(dummy)
            if SCAN_CUT == -1:
                P.barrier()
                P.release(dummy)
                return
            S_t = [st.enter_context(nc.sbuf_tensor(f"S{hp}d{dr}", [128, 64], BF16)) for hp in range(8)]
            S_b = [[Buf(S_t[hp], f"S{hp}_{hh}") for hh in range(2)] for hp in range(8)]
            for hp in range(8):
                for hh in range(2):
                    P.op("pool", lambda e: e.memset(S_t[hp][hh * 64:(hh + 1) * 64, :], 0.0), writes=[S_b[hp][hh]])
            HB = sb(st, "halo", [128, TB + 2], F32, n=4)
            ctmp = sb(st, "ctmp", [128, TB], F32)
            cw = sb(st, "cw", [128, TB], F32, n=3)
            tw = sb(st, "tw", [128, TB], BF16, n=2)
            adb = sb(st, "adb", [128, TB], BF16, n=2)
            sgd = sb(st, "sgd", [128, TB], BF16, n=2)
            F = {nm: sb(st, "f_" + nm, [128, TB], F32, n=2) for nm in
                 ("r", "k", "v", "a", "ao", "sq", "rn", "kk", "t", "kd", "kdo", "b", "e1", "e2", "e3", "e4", "dd", "m", "yc", "bon")}
            Bf = {nm: sb(st, "b_" + nm, [128, TB], BF16, n=2) for nm in ("at", "rt", "bt", "kt", "bh", "kh", "vb", "ry")}
            sgw = sb(st, "sgw", [128, 4, 128], F32, n=2)
            wend = sb(st, "wend", [128, 8], F32, n=2)
            Tk = {nm: sb(st, "t_" + nm, [128, 4, 128], BF16, n=2) for nm in ("A", "Bh", "Kh", "V")}
            ATs = sb(st, "ATs", [128, 4, 128], BF16, n=8)
            Nt = sb(st, "Nt", [128, 128], BF16, n=8)
            Xl = sb(st, "Xl", [128, 2, 128], BF16, n=40)
            Zr = sb(st, "Zr", [128, 128], BF16, n=24)
            Y0 = sb(st, "Y0", [128, 64], F32, n=8)
            YGT_t = [st.enter_context(nc.sbuf_tensor(f"YGT{i}d{dr}", [128, 128], BF16)) for i in range(4)]
            PT_t = [st.enter_context(nc.sbuf_tensor(f"PT{i}d{dr}", [128, 128], BF16)) for i in range(4)]
            Q_t = [st.enter_context(nc.sbuf_tensor(f"Qt{i}d{dr}", [128, 128], F32)) for i in range(4)]
            ytile = sb(st, "ytile", [128, 4, 128], F32, n=2)
            yfl = sb(st, "yfl", [128, 4, 128], F32, n=2)
            ysq = sb(st, "ysq", [128, 4, 128], F32, n=2)
            ynb = sb(st, "ynb", [128, 4, 128], BF16, n=2)
            st8 = {nm: sb(st, "s8" + nm, [128, 8], F32, n=2) for nm in ("s1", "s2", "mu", "var")}
            pA = ps(st, "pA", [128, 512], F32, n=2)
            pT = ps(st, "pT", [128, 4, 128], BF16, n=2)
            pU = ps(st, "pU", [128, 512], F32, n=4)
            ring_i = [0]

            slots = range(NSLOT) if dr == 0 else range(NSLOT - 1, -1, -1)
            if SCAN_CUT <= 1:
                slots = []
            for s in slots:
                first = (s == 0) if dr == 0 else (s == NSLOT - 1)
                if not first:
                    cs = s if dr == 0 else s + 1
                    for hp in range(8):
                        for hh in range(2):
                            P.op("pool", lambda e: e.tensor_scalar(S_t[hp][hh * 64:(hh + 1) * 64, :], S_t[hp][hh * 64:(hh + 1) * 64, :],
                                                                   carry_t[hh * 64:(hh + 1) * 64, cs:cs + 1], None, ALU.mult),
                                 reads=[carry_t], writes=[S_b[hp][hh]])
                blocks = range(BPS) if dr == 0 else range(BPS - 1, -1, -1)
                for bi in blocks:
                    b = s * BPS + bi
                    tok0 = b * TB
                    cws = []
                    for c in (24, 25, 26):
                        hb = HB.next()
                        load_halo(hb, prow(8 + c), PROJ_B[8 + c], tok0, TB, 1)
                        o = cw.next()
                        conv3("pool", o, hb, shw, c, TB, [], tmp=ctmp)
                        cws.append(o)
                    tw_ = tw.next(); adb_ = adb.next(); sgd_ = sgd.next()
                    P.op("act", lambda e: e.activation(tw_[:], cws[0][:], AF.Tanh), reads=[cws[0]], writes=[tw_])
                    P.op("dve", lambda e: e.tensor_copy(adb_[:], cws[1][:]), reads=[cws[1]], writes=[adb_])
                    P.op("act", lambda e: e.activation(sgd_[:], cws[2][:], AF.Sigmoid), reads=[cws[2]], writes=[sgd_])
                    for hp in range(8):
                        hs = slice(hp * 128, (hp + 1) * 128)
                        f = {k: v.next() for k, v in F.items()}
                        bq = {k: v.next() for k, v in Bf.items()}
                        for nm, c in (("r", hp), ("k", 8 + hp), ("v", 16 + hp)):
                            hb = HB.next()
                            load_halo(hb, prow(8 + c), PROJ_B[8 + c], tok0, TB, 1)
                            conv3("dve" if nm != "v" else "pool", f[nm], hb, shw, c, TB, [], tmp=ctmp)
                        def a_of(d_, dst):
                            p_ = pA.next()
                            P.op("pe", lambda e: e.matmul(p_[:], lw["a_up"][d_ * 64:(d_ + 1) * 64, hs], adb_[d_ * 64:(d_ + 1) * 64, :],
                                                          start=True, stop=True), reads=[lw["a_up"], adb_], writes=[p_])
                            P.op("act", lambda e: e.activation(dst[:], p_[:], AF.Sigmoid, bias=vec["a0"][:, d_, hp:hp + 1]),
                                 reads=[p_, vec["a0"]], writes=[dst])
                        a_of(dr, f["a"])
                        p_ = pA.next()
                        for i in range(4):
                            P.op("pe", lambda e: e.matmul(p_[:, i * 128:(i + 1) * 128], tw_[dr * 64:(dr + 1) * 64, i * 128:(i + 1) * 128],
                                                          lw["w_up"][dr * 64:(dr + 1) * 64, hs], start=True, stop=True),
                                 reads=[tw_, lw["w_up"]], writes=[p_], inc=(i == 3))
                        sgw_ = sgw.next()
                        P.op("dve", lambda e: e.tensor_tensor(sgw_[:], p_[:].rearrange("p (i c) -> p i c", i=4),
                                                              w0b[:, hs].unsqueeze(1).to_broadcast([128, 4, 128]), ALU.add),
                             reads=[p_, w0b], writes=[sgw_])
                        P.op("act", lambda e: e.activation(sgw_[:], sgw_[:], AF.Sigmoid), reads=[sgw_], writes=[sgw_])
                        def cum(v):
                            p2 = pA.next()
                            for i in range(4):
                                P.op("pe", lambda e: e.matmul(p2[:, i * 128:(i + 1) * 128], sgw_[:, i, :], tri[:, v * 128:(v + 1) * 128],
                                                              start=True, stop=True), reads=[sgw_, tri], writes=[p2], inc=(i == 3))
                            return p2
                        pc = cum(0)
                        P.op("act", lambda e: e.activation(f["e1"][:], pc[:], AF.Exp), reads=[pc], writes=[f["e1"]])
                        P.op("act", lambda e: e.activation(f["e3"][:], pc[:], AF.Exp, scale=-1.0), reads=[pc], writes=[f["e3"]])
                        P.op("dve", lambda e: e.tensor_copy(f["dd"][:], pc[:]), reads=[pc], writes=[f["dd"]])
                        pc = cum(1)
                        P.op("act", lambda e: e.activation(f["e2"][:], pc[:], AF.Exp), reads=[pc], writes=[f["e2"]])
                        pc = cum(2)
                        wend_ = wend.next()
                        P.op("act", lambda e: e.activation(wend_[:], pc[:].rearrange("p (j c) -> p j c", c=64)[:, :, 0], AF.Exp),
                             reads=[pc], writes=[wend_])
                        P.op("dve", lambda e: e.tensor_tensor(f["dd"][:], pc[:], f["dd"][:], ALU.subtract), reads=[pc, f["dd"]], writes=[f["dd"]])
                        P.op("act", lambda e: e.activation(f["e4"][:], f["dd"][:], AF.Exp), reads=[f["dd"]], writes=[f["e4"]])
                        P.op("act", lambda e: e.activation(f["sq"][:], f["k"][:], AF.Square, scale=vec["k_k"][:, hp:hp + 1]),
                             reads=[f["k"], vec["k_k"]], writes=[f["sq"]])
                        p3 = pA.next()
                        P.op("pe", lambda e: e.matmul(p3[:], bones[:], f["sq"][:], start=True, stop=True), reads=[bones, f["sq"]], writes=[p3])
                        P.op("act", lambda e: e.sqrt(f["rn"][:], p3[:]), reads=[p3], writes=[f["rn"]])
                        P.op("dve", lambda e: e.tensor_scalar(f["rn"][:], f["rn"][:], 1e-12, None, ALU.max), reads=[f["rn"]], writes=[f["rn"]])
                        P.op("dve", lambda e: e.reciprocal(f["rn"][:], f["rn"][:]), reads=[f["rn"]], writes=[f["rn"]])
                        P.op("dve", lambda e: e.scalar_tensor_tensor(f["kk"][:], f["k"][:], vec["k_k"][:, hp:hp + 1], f["rn"][:], ALU.mult, ALU.mult),
                             reads=[f["k"], vec["k_k"], f["rn"]], writes=[f["kk"]])
                        def kdir(asrc, dst):
                            P.op("dve", lambda e: e.tensor_scalar(f["t"][:], asrc[:], vec["k_a"][:, hp:hp + 1], omka[:, hp:hp + 1], ALU.mult, ALU.add),
                                 reads=[asrc, vec["k_a"], omka], writes=[f["t"]])
                            P.op("dve", lambda e: e.tensor_tensor(dst[:], f["k"][:], f["t"][:], ALU.mult), reads=[f["k"], f["t"]], writes=[dst])
                        kdir(f["a"], f["kd"])
                        P.op("pool", lambda e: e.tensor_tensor(f["b"][:], f["kk"][:], f["a"][:], ALU.mult), reads=[f["kk"], f["a"]], writes=[f["b"]])
                        P.op("dve", lambda e: e.scalar_tensor_tensor(bq["at"][:], f["kk"][:], -1.0, f["e2"][:], ALU.mult, ALU.mult),
                             reads=[f["kk"], f["e2"]], writes=[bq["at"]])
                        P.op("pool", lambda e: e.tensor_tensor(bq["rt"][:], f["r"][:], f["e1"][:], ALU.mult), reads=[f["r"], f["e1"]], writes=[bq["rt"]])
                        P.op("dve", lambda e: e.tensor_tensor(bq["bt"][:], f["b"][:], f["e3"][:], ALU.mult), reads=[f["b"], f["e3"]], writes=[bq["bt"]])
                        P.op("pool", lambda e: e.tensor_tensor(bq["kt"][:], f["kd"][:], f["e3"][:], ALU.mult), reads=[f["kd"], f["e3"]], writes=[bq["kt"]])
                        P.op("dve", lambda e: e.tensor_tensor(bq["bh"][:], f["b"][:], f["e4"][:], ALU.mult), reads=[f["b"], f["e4"]], writes=[bq["bh"]])
                        P.op("pool", lambda e: e.tensor_tensor(bq["kh"][:], f["kd"][:], f["e4"][:], ALU.mult), reads=[f["kd"], f["e4"]], writes=[bq["kh"]])
                        P.op("act", lambda e: e.copy(bq["vb"][:], f["v"][:]), reads=[f["v"]], writes=[bq["vb"]])
                        tk = {}
                        for k2, (nm, src) in enumerate((("A", "at"), ("Bh", "bh"), ("Kh", "kh"), ("V", "vb"))):
                            pt_ = pT.next()
                            for i in range(4):
                                P.op("pe", lambda e: e.transpose(pt_[:, i, :], bq[src][:, i * 128:(i + 1) * 128], ident_b[:]),
                                     reads=[bq[src], ident_b], writes=[pt_], inc=(i == 3))
                            tk[nm] = Tk[nm].next()
                            P.op("act" if k2 % 2 else "dve",
                                 (lambda e: e.copy(tk[nm][:], pt_[:])) if k2 % 2 else (lambda e: e.tensor_copy(tk[nm][:], pt_[:])),
                                 reads=[pt_], writes=[tk[nm]])
                        if SCAN_CUT <= 2:
                            continue
                        yt_ = ytile.next()
                        yt_parts = [[Buf(yt_.t, "ytp") for hh in range(2)] for i in range(4)]
                        for i in range(4):
                            for hh in range(2):
                                yt_parts[i][hh].w = dict(yt_.w); yt_parts[i][hh].r = dict(yt_.r)

                        def unit(i, hh, slot_idx):
                            cs_ = slice(hh * 64, (hh + 1) * 64)
                            ts_ = slice(i * 128, (i + 1) * 128)
                            Sb = S_b[hp][hh]
                            St = S_t[hp]
                            ats = ATs.next()
                            pu = pU.next()
                            for n2, (l_, r_) in enumerate((("bt", "at"), ("bt", "rt"), ("kt", "at"), ("kt", "rt"))):
                                P.op("pe", lambda e: e.matmul(pu[:, n2 * 128:(n2 + 1) * 128], bq[l_][cs_, ts_], bq[r_][cs_, ts_], start=True, stop=True),
                                     reads=[bq[l_], bq[r_]], writes=[pu], inc=(n2 == 3))
                            P.op("dve", lambda e: e.tensor_tensor(ats[:].rearrange("p (a b) c -> p a (b c)", a=2),
                                                                  pu[:].rearrange("p (a c) -> p a c", a=2),
                                                                  msk[:, 0:256].unsqueeze(1).to_broadcast([128, 2, 256]), ALU.mult),
                                 reads=[pu, msk], writes=[ats])
                            pu = pU.next()
                            P.op("pe", lambda e: e.matmul(pu[:, 0:128], bq["at"][cs_, ts_], bq["bt"][cs_, ts_], start=True, stop=True),
                                 reads=[bq["at"], bq["bt"]], writes=[pu])
                            nt = Nt.next()
                            P.op("act", lambda e: e.activation(nt[:], pu[:, 0:128], AF.Copy), reads=[pu], writes=[nt])
                            P.op("pool", lambda e: e.tensor_tensor(nt[:], nt[:], msk[:, 256:384], ALU.mult), reads=[msk], writes=[nt])
                            yield
                            z = Zr.next()
                            pu = pU.next()
                            P.op("pe", lambda e: e.matmul(pu[:, 0:64], ats[:, 2, :], tk["V"][:, i, cs_], start=True, stop=True),
                                 reads=[ats, tk["V"]], writes=[pu])
                            P.op("act", lambda e: e.copy(z[:, 64:128], pu[:, 0:64]), reads=[pu], writes=[z])
                            P.op("pool", lambda e: e.tensor_copy(z[:, 0:64], tk["A"][:, i, cs_]), reads=[tk["A"]], writes=[z])
                            yield
                            X = [None] * 6
                            XT = [None] * 6
                            X[0] = (nt, nt[:]); XT[0] = (ats, ats[:, 0, :])
                            for j in range(1, 6):
                                pu = pU.next()
                                if j < 5:
                                    P.op("pe", lambda e: e.matmul(pu[:, 0:128], XT[j - 1][1], X[j - 1][1], start=True, stop=True),
                                         reads=[XT[j - 1][0], X[j - 1][0]], writes=[pu], inc=False)
                                P.op("pe", lambda e: e.matmul(pu[:, 128:256], X[j - 1][1], XT[j - 1][1], start=True, stop=True),
                                     reads=[XT[j - 1][0], X[j - 1][0]], writes=[pu])
                                xl = Xl.next()
                                if j < 5:
                                    P.op("act" if j % 2 else "dve",
                                         (lambda e: e.copy(xl[:].rearrange("p a c -> p (a c)"), pu[:, 0:256])) if j % 2 else
                                         (lambda e: e.tensor_copy(xl[:].rearrange("p a c -> p (a c)"), pu[:, 0:256])),
                                         reads=[pu], writes=[xl])
                                else:
                                    P.op("act", lambda e: e.copy(xl[:, 1, :], pu[:, 128:256]), reads=[pu], writes=[xl])
                                X[j] = (xl, xl[:, 0, :]); XT[j] = (xl, xl[:, 1, :])
                                yield
                            for j in range(5, -1, -1):
                                pu = pU.next()
                                P.op("pe", lambda e: e.matmul(pu[:, 0:128], ident_b[:], z[:], start=True, stop=False),
                                     reads=[ident_b, z], writes=[pu], inc=False)
                                P.op("pe", lambda e: e.matmul(pu[:, 0:128], XT[j][1], z[:], start=False, stop=True),
                                     reads=[XT[j][0], z], writes=[pu])
                                zn = Zr.next()
                                if j % 2:
                                    P.op("act", lambda e: e.copy(zn[:], pu[:, 0:128]), reads=[pu], writes=[zn])
                                else:
                                    P.op("dve", lambda e: e.tensor_copy(zn[:], pu[:, 0:128]), reads=[pu], writes=[zn])
                                z = zn
                                yield
                            G = z[:, 0:64]
                            U0 = z[:, 64:128]
                            pu = pU.next()
                            P.op("pe", lambda e: e.matmul(pu[:, 0:64], ats[:, 1, :], U0, start=True, stop=False), reads=[ats, z], writes=[pu], inc=False)
                            P.op("pe", lambda e: e.matmul(pu[:, 0:64], ats[:, 3, :], tk["V"][:, i, cs_], start=False, stop=True), reads=[ats, tk["V"]], writes=[pu])
                            y0 = Y0.next()
                            P.op("act", lambda e: e.copy(y0[:], pu[:, 0:64]), reads=[pu], writes=[y0])
                            pu = pU.next()
                            pub = pU.next()
                            pq = [pu, pub]
                            P.op("pe", lambda e: e.matmul(pu[cs_, 0:128], G, ats[:, 1, :], start=True, stop=True), reads=[z, ats], writes=[pu], inc=False)
                            for q in range(2):
                                qs = slice(q * 64, (q + 1) * 64)
                                P.op("pe", lambda e: e.matmul(pq[q][cs_, 128 + q * 64:192 + q * 64], z[qs, 0:64], tk["Bh"][qs, i, cs_], start=True, stop=True),
                                     reads=[z, tk["Bh"]], writes=[pq[q]], inc=True)
                            ygt = Buf(YGT_t[slot_idx % 4], "ygt"); ptb = Buf(PT_t[slot_idx % 4], "pt"); qb = Buf(Q_t[slot_idx % 4], "q")
                            ygt.w = dict(prev[slot_idx % 4][hh]); ptb.w = dict(prev[slot_idx % 4][hh]); qb.w = dict(prev[slot_idx % 4][hh])
                            P.op("dve", lambda e: e.tensor_tensor(ygt.t[cs_, :], pu[cs_, 0:128], bq["rt"][cs_, ts_], ALU.add), reads=[pu, bq["rt"]], writes=[ygt])
                            for q in range(2):
                                P.op("dve", lambda e: e.scalar_tensor_tensor(ptb.t[cs_, q * 64:(q + 1) * 64], ident_f[cs_, cs_],
                                                                             wend_[cs_, 2 * i + q:2 * i + q + 1], pq[q][cs_, 128 + q * 64:192 + q * 64],
                                                                             ALU.mult, ALU.add), reads=[ident_f, wend_, pq[q]], writes=[ptb])
                            pq = [pU.next(), pU.next()]
                            for q in range(2):
                                qs = slice(q * 64, (q + 1) * 64)
                                P.op("pe", lambda e: e.matmul(pq[q][cs_, q * 64:(q + 1) * 64], tk["Bh"][qs, i, cs_], z[qs, 64:128], start=True, stop=False),
                                     reads=[tk["Bh"], z], writes=[pq[q]], inc=False)
                                P.op("pe", lambda e: e.matmul(pq[q][cs_, q * 64:(q + 1) * 64], tk["Kh"][qs, i, cs_], tk["V"][qs, i, cs_], start=False, stop=True),
                                     reads=[tk["Kh"], tk["V"]], writes=[pq[q]], inc=True)
                                P.op("act", lambda e: e.copy(qb.t[cs_, q * 64:(q + 1) * 64], pq[q][cs_, q * 64:(q + 1) * 64]), reads=[pq[q]], writes=[qb])
                            yield
                            puy = pU.next()
                            for q in ((0, 1) if dr == 0 else (1, 0)):
                                qs = slice(q * 64, (q + 1) * 64)
                                P.op("pe", lambda e: e.matmul(puy[qs, 0:64], ygt.t[cs_, qs], St[cs_, :], start=True, stop=True),
                                     reads=[ygt, Sb], writes=[puy])
                                pus = pU.next()
                                P.op("pe", lambda e: e.matmul(pus[cs_, 0:64], ptb.t[cs_, qs], St[cs_, :], start=True, stop=True),
                                     reads=[ptb, Sb], writes=[pus])
                                P.op("dve", lambda e: e.tensor_tensor(St[cs_, :], pus[cs_, 0:64], qb.t[cs_, qs], ALU.add),
                                     reads=[pus, qb], writes=[Sb])
                            P.op("dve", lambda e: e.tensor_tensor(yt_.t[:, i, cs_], puy[:, 0:64], y0[:], ALU.add),
                                 reads=[puy, y0], writes=[yt_parts[i][hh]])
                            d = {}
                            for bb_ in (ygt, ptb, qb):
                                for tok in list(bb_.r.values()) + list(bb_.w.values()):
                                    Prog._add(d, tok)
                            prev[slot_idx % 4][hh] = d
                            yield

                        tiles = range(4) if dr == 0 else range(3, -1, -1)
                        gens = []
                        for ui, i in enumerate(tiles):
                            for hh in range(2):
                                gens.append(unit(i, hh, ui))
                        alive = list(gens)
                        while alive:
                            nxt = []
                            for g_ in alive:
                                try:
                                    next(g_)
                                    nxt.append(g_)
                                except StopIteration:
                                    pass
                            alive = nxt
                        yt_.w = {}
                        yt_.r = {}
                        for i in range(4):
                            for hh in range(2):
                                for tok in yt_parts[i][hh].w.values():
                                    Prog._add(yt_.w, tok)
                        t0 = tok0
                        if dr == 0:
                            P.dma(YF[t0:t0 + TB, hs].rearrange("(i p) c -> p i c", p=128), yt_[:], reads=[yt_], writes=[YF_B[b][hp]], owner=yt_)
                            continue
                        yf_ = yfl.next()
                        P.dma(yf_[:], YF[t0:t0 + TB, hs].rearrange("(i p) c -> p i c", p=128), reads=[YF_B[b][hp]], writes=[yf_])
                        P.op("pool", lambda e: e.tensor_tensor(yt_[:], yt_[:], yf_[:], ALU.add), reads=[yf_], writes=[yt_])
                        s8 = {k: v.next() for k, v in st8.items()}
                        ysq_ = ysq.next(); ynb_ = ynb.next()
                        v8 = lambda ap: ap.rearrange("p i (h c) -> p (i h) c", h=2)
                        P.op("dve", lambda e: e.tensor_reduce(s8["s1"][:], v8(yt_[:]), axis=AX.X, op=ALU.add), reads=[yt_], writes=[s8["s1"]])
                        P.op("act", lambda e: e.activation(ysq_[:], yt_[:], AF.Square), reads=[yt_], writes=[ysq_])
                        P.op("dve", lambda e: e.tensor_reduce(s8["s2"][:], v8(ysq_[:]), axis=AX.X, op=ALU.add), reads=[ysq_], writes=[s8["s2"]])
                        P.op("dve", lambda e: e.tensor_scalar(s8["mu"][:], s8["s1"][:], 1.0 / 64, None, ALU.mult), reads=[s8["s1"]], writes=[s8["mu"]])
                        P.op("dve", lambda e: e.tensor_tensor(s8["var"][:], s8["mu"][:], s8["mu"][:], ALU.mult), reads=[s8["mu"]], writes=[s8["var"]])
                        P.op("dve", lambda e: e.scalar_tensor_tensor(s8["var"][:], s8["s2"][:], 1.0 / 64, s8["var"][:], ALU.mult, ALU.subtract),
                             reads=[s8["s2"]], writes=[s8["var"]])
                        P.op("dve", lambda e: e.tensor_scalar(s8["var"][:], s8["var"][:], LNX_EPS, None, ALU.add), writes=[s8["var"]])
                        P.op("act", lambda e: e.sqrt(s8["var"][:], s8["var"][:]), writes=[s8["var"]])
                        P.op("dve", lambda e: e.reciprocal(s8["var"][:], s8["var"][:]), writes=[s8["var"]])
                        P.op("dve", lambda e: e.tensor_tensor(v8(yt_[:]), v8(yt_[:]), s8["mu"][:].unsqueeze(2).to_broadcast([128, 8, 64]), ALU.subtract),
                             reads=[s8["mu"]], writes=[yt_])
                        P.op("dve", lambda e: e.tensor_tensor(v8(ynb_[:]), v8(yt_[:]), s8["var"][:].unsqueeze(2).to_broadcast([128, 8, 64]), ALU.mult),
                             reads=[yt_, s8["var"]], writes=[ynb_])
                        pt_ = pT.next()
                        for i in range(4):
                            P.op("pe", lambda e: e.transpose(pt_[:, i, :], ynb_[:, i, :], ident_b[:]), reads=[ynb_, ident_b], writes=[pt_], inc=(i == 3))
                        P.op("dve", lambda e: e.tensor_scalar(f["yc"][:], pt_[:].rearrange("p i c -> p (i c)"), vec["lnx_w"][:, hp:hp + 1],
                                                              vec["lnx_b"][:, hp:hp + 1], ALU.mult, ALU.add),
                             reads=[pt_, vec["lnx_w"], vec["lnx_b"]], writes=[f["yc"]])
                        a_of(od, f["ao"])
                        kdir(f["ao"], f["kdo"])
                        P.op("pool", lambda e: e.tensor_scalar(f["m"][:], f["kd"][:], vec["r_k"][:, dr, hp:hp + 1], None, ALU.mult),
                             reads=[f["kd"], vec["r_k"]], writes=[f["m"]])
                        P.op("dve", lambda e: e.scalar_tensor_tensor(f["m"][:], f["kdo"][:], vec["r_k"][:, od, hp:hp + 1], f["m"][:], ALU.mult, ALU.add),
                             reads=[f["kdo"], vec["r_k"]], writes=[f["m"]])
                        P.op("pool", lambda e: e.tensor_tensor(f["m"][:], f["m"][:], f["r"][:], ALU.mult), reads=[f["r"]], writes=[f["m"]])
                        p4 = pA.next()
                        P.op("pe", lambda e: e.matmul(p4[:], bones[:], f["m"][:], start=True, stop=True), reads=[bones, f["m"]], writes=[p4])
                        P.op("dve", lambda e: e.tensor_tensor(f["bon"][:], p4[:], f["v"][:], ALU.mult), reads=[p4, f["v"]], writes=[f["bon"]])
                        P.op("pool", lambda e: e.tensor_tensor(f["yc"][:], f["yc"][:], f["bon"][:], ALU.add), reads=[f["bon"]], writes=[f["yc"]])
                        p5 = pA.next()
                        P.op("pe", lambda e: e.matmul(p5[:], lw["g_up"][:, hs], sgd_[:], start=True, stop=True), reads=[lw["g_up"], sgd_], writes=[p5])
                        P.op("dve", lambda e: e.tensor_tensor(bq["ry"][:], p5[:], f["yc"][:], ALU.mult), reads=[p5, f["yc"]], writes=[bq["ry"]])
                        P.dma(RY[hs, t0:t0 + TB], bq["ry"][:], reads=[bq["ry"]], writes=[RY_B[hp][b]], owner=bq["ry"])
            P.barrier()
            P.release(dummy)

    prev = [[{}, {}] for _ in range(4)]
    if upto >= 2:
        scan_pass(0)
    prev = [[{}, {}] for _ in range(4)]
    if upto >= 3:
        scan_pass(1)

    X1 = dscr("s_x1", [TT, D])
    X1_B = [Buf(None, f"x1_{t}") for t in range(TT // 128)]
    H2T = dscr("s_h2t", [D, TT], BF16)
    H2T_B = [Buf(None, f"h2t_{b}") for b in range(NB)]
    gffn = CVW["norm_ffn_g"]

    def rms_rstd(x, s_, r_, junk):
        P.op("act", lambda e: e.activation(junk[:], x[:], AF.Square, accum_out=s_[:]), reads=[x], writes=[junk, s_])
        P.op("dve", lambda e: e.tensor_scalar(r_[:], s_[:], 1.0 / D, EPS, ALU.mult, ALU.add), reads=[s_], writes=[r_])
        P.op("act", lambda e: e.sqrt(r_[:], r_[:]), reads=[r_], writes=[r_])
        P.op("dve", lambda e: e.reciprocal(r_[:], r_[:]), reads=[r_], writes=[r_])

    def stage4():
        convert_weights(["p_pool", "p_rwkv", "w_out"])
        with ExitStack() as st:
            pws = sb(st, "pws", [128, 4, 2, 256], BF16)
            psc = CVW["pool_scale"]
            with ExitStack() as st2:
                pwf = sb(st2, "pwf", [128, 4, 2, 256])
                P.dma(pwf[:], W["pool_w"].rearrange("g (cc p) d -> p g cc d", p=128), writes=[pwf])
                P.op("dve", lambda e: e.tensor_copy(pws[:], pwf[:]), reads=[pwf], writes=[pws])
                P.barrier()
            P.release(dummy)
            wpp = sb(st, "wpp", [128, 8, 512], BF16, n=1, ring=True)
            wpr = sb(st, "wpr", [128, 8, 512], BF16, n=1, ring=True)
            wo = sb(st, "wo", [128, 16, 512], BF16, n=2)
            WD = TB + 16
            hbp = sb(st, "hbp", [128, WD], F32, n=2)
            acc = [sb(st, f"acc{k}", [128, WD], F32, n=1, ring=True) for k in range(4)]
            invc = [sb(st, f"invc{g}", [128, TB], F32, n=1) for g in range(4)]
            pooled = [sb(st, f"pooled{c}", [128, TB], BF16, n=1) for c in range(8)]
            tmpw = sb(st, "tmpw", [128, TB], F32, n=2)
            mixed = [sb(st, f"mixed{c}", [128, TB], BF16, n=1) for c in range(8)]
            ryb = [sb(st, f"ryb{c}", [128, TB], BF16, n=1) for c in range(8)]
            gate = sb(st, "gate", [128, TB], F32, n=2)
            mp = sb(st, "mp", [128, TB], F32, n=2)
            merged = sb(st, "merged", [128, 16, TB], BF16, n=1)
            xt = sb(st, "x4", [128, D], F32, n=4)
            junk = sb(st, "junk4", [128, D], BF16, n=1)
            xs = sb(st, "xs4", [128, D], BF16, n=1, ring=True)
            ss = sb(st, "ss4", [128, 1], F32, n=2)
            rstd = sb(st, "rstd4", [128, 1], F32, n=2)
            h2 = sb(st, "h2", [128, 16, TB], BF16, n=1, ring=True)
            pg = ps(st, "pg4", [128, 512], F32, n=5)
            ptr = ps(st, "ptr4", [128, 16, 128], BF16, n=1)
            for b in range(NB):
                tok0 = b * TB
                for g in range(4):
                    P.dma(invc[g][:], W["invcnt"][g, :, tok0:tok0 + TB], writes=[invc[g]])
                for c in range(8):
                    g = c // 2
                    hb = hbp.next()
                    load_halo(hb, prow(c), PROJ_B[c], tok0, TB, 8)
                    a = [acc[k].next() for k in range(4)]
                    eng = "pool" if c % 2 else "dve"
                    P.op(eng, lambda e: e.tensor_tensor(a[0][:, 1:WD], hb[:, 0:WD - 1], hb[:, 1:WD], ALU.add), reads=[hb], writes=[a[0]])
                    if g >= 1:
                        P.op(eng, lambda e: e.tensor_tensor(a[1][:, 2:WD - 1], a[0][:, 1:WD - 2], a[0][:, 3:WD], ALU.add), reads=[a[0]], writes=[a[1]])
                    if g >= 2:
                        P.op(eng, lambda e: e.tensor_tensor(a[2][:, 4:WD - 3], a[1][:, 2:WD - 5], a[1][:, 6:WD - 1], ALU.add), reads=[a[1]], writes=[a[2]])
                    if g >= 3:
                        P.op(eng, lambda e: e.tensor_tensor(a[3][:, 8:WD - 7], a[2][:, 4:WD - 11], a[2][:, 12:WD - 3], ALU.add), reads=[a[2]], writes=[a[3]])
                    win = a[g]
                    t_ = tmpw.next()
                    P.op(eng, lambda e: e.tensor_tensor(t_[:], win[:, 8:8 + TB], invc[g][:], ALU.mult), reads=[win, invc[g]], writes=[t_])
                    P.op(eng, lambda e: e.tensor_tensor(pooled[c][:], t_[:], hb[:, 8:8 + TB], ALU.subtract), reads=[t_, hb], writes=[pooled[c]])
                for dc in range(8):
                    g, dd = dc // 2, dc % 2
                    p_ = pg.next()
                    for cc in range(2):
                        P.op("pe", lambda e: e.matmul(p_[:], pws[:, g, cc, dd * 128:(dd + 1) * 128], pooled[2 * g + cc][:], start=(cc == 0), stop=(cc == 1)),
                             reads=[pws, pooled[2 * g + cc]], writes=[p_], inc=(cc == 1))
                    P.op("act", lambda e: e.activation(mixed[dc][:], p_[:], AF.Copy, scale=psc[:, dc:dc + 1]), reads=[p_, psc], writes=[mixed[dc]])
                for c in range(8):
                    P.dma(ryb[c][:], RY[c * 128:(c + 1) * 128, tok0:tok0 + TB], reads=[RY_B[c][b]], writes=[ryb[c]])
                for g4 in range(4):
                    wp_ = wpp.next(); wr_ = wpr.next()
                    P.dma(wp_[:], WS["p_pool"]["ap"][g4], reads=[WS["p_pool"]["bufs"][g4]], writes=[wp_])
                    P.dma(wr_[:], WS["p_rwkv"]["ap"][g4], reads=[WS["p_rwkv"]["bufs"][g4]], writes=[wr_])
                    for jj in range(4):
                        e_ = g4 * 4 + jj
                        gp = gate.next(); gr = gate.next()
                        P.dma(gp[:], prow(35 + e_)[:, tok0:tok0 + TB], reads=[PROJ_B[35 + e_][b]], writes=[gp])
                        P.dma(gr[:], prow(51 + e_)[:, tok0:tok0 + TB], reads=[PROJ_B[51 + e_][b]], writes=[gr])
                        p1 = pg.next()
                        for c in range(8):
                            P.op("pe", lambda e: e.matmul(p1[:], wp_[:, c, jj * 128:(jj + 1) * 128], mixed[c][:], start=(c == 0), stop=(c == 7)),
                                 reads=[wp_, mixed[c]], writes=[p1], inc=(c == 7))
                        p2 = pg.next()
                        for c in range(8):
                            P.op("pe", lambda e: e.matmul(p2[:], wr_[:, c, jj * 128:(jj + 1) * 128], ryb[c][:], start=(c == 0), stop=(c == 7)),
                                 reads=[wr_, ryb[c]], writes=[p2], inc=(c == 7))
                        m_ = mp.next()
                        P.op("dve", lambda e: e.tensor_tensor(m_[:], p1[:], gp[:], ALU.mult), reads=[p1, gp], writes=[m_])
                        P.op("dve", lambda e: e.tensor_tensor(gr[:], p2[:], gr[:], ALU.mult), reads=[p2], writes=[gr])
                        P.op("pool", lambda e: e.tensor_tensor(merged[:, e_, :], m_[:], gr[:], ALU.add), reads=[m_, gr], writes=[merged])
                h = h2.next()
                xl = []
                for i in range(4):
                    t0 = tok0 + i * 128
                    x = xt.next()
                    P.dma(x[:], x_in[t0:t0 + 128, :], writes=[x])
                    xl.append(x)
                for n in range(4):
                    w_ = wo.next()
                    P.dma(w_[:], WS["w_out"]["ap"][n], reads=[WS["w_out"]["bufs"][n]], writes=[w_])
                    for i in range(4):
                        x = xl[i]
                        p_ = pg.next()
                        for kc in range(16):
                            P.op("pe", lambda e: e.matmul(p_[:], merged[:, kc, i * 128:(i + 1) * 128], w_[:, kc, :], start=(kc == 0), stop=(kc == 15)),
                                 reads=[merged, w_], writes=[p_], inc=(kc == 15))
                        P.op("dve", lambda e: e.tensor_tensor(x[:, n * 512:(n + 1) * 512], p_[:], x[:, n * 512:(n + 1) * 512], ALU.add), reads=[p_], writes=[x])
                for i in range(4):
                    t0 = tok0 + i * 128
                    x = xl[i]
                    P.dma(X1[t0:t0 + 128, :], x[:], reads=[x], writes=[X1_B[t0 // 128]], owner=x)
                    s_ = ss.next(); r_ = rstd.next(); xs_ = xs.next()
                    rms_rstd(x, s_, r_, junk)
                    P.op("act", lambda e: e.activation(xs_[:], x[:], AF.Copy, scale=r_[:]), reads=[x, r_], writes=[xs_])
                    for kc in range(16):
                        P.op("pe", lambda e: e.transpose(ptr[:, kc, :], xs_[:, kc * 128:(kc + 1) * 128], ident_b[:]),
                             reads=[xs_, ident_b], writes=[ptr], inc=(kc == 15))
                    P.op("dve", lambda e: e.tensor_tensor(h[:, :, i * 128:(i + 1) * 128], ptr[:],
                                                          gffn[:].unsqueeze(2).to_broadcast([128, 16, 128]), ALU.mult),
                         reads=[ptr, gffn], writes=[h])
                P.dma(H2T.rearrange("(kc p) t -> p kc t", p=128)[:, :, tok0:tok0 + TB], h[:], reads=[h], writes=[H2T_B[b]], owner=h)
            P.barrier()
            P.release(dummy)

    if upto >= 4:
        stage4()

    def stage5():
        convert_weights(["ffn_up", "ffn_down"])
        with ExitStack() as st:
            fcw = CVW["ffn_conv_w"]
            fcb = CVW["ffn_conv_b"]
            gfin = sb(st, "gfin", [128, D])
            P.dma(gfin[:], W["norm_final_g"], writes=[gfin])
            h2 = sb(st, "h2f", [128, 16, TB + 2], BF16, n=1)
            wup = sb(st, "wup", [128, 16, 512], BF16, n=2)
            wdn = sb(st, "wdn", [128, 22, 512], BF16, n=2)
            U = sb(st, "U", [128, TB + 2], F32, n=3)
            cvg = sb(st, "cvg", [128, TB], F32, n=2)
            cvv = sb(st, "cvv", [128, TB], F32, n=2)
            ctmp5 = sb(st, "ctmp5", [128, TB], F32)
            actT = sb(st, "actT", [128, 44, TB], BF16, n=1)
            x1t = [sb(st, f"x1t{i}", [128, D], F32, n=1) for i in range(4)]
            junk = sb(st, "junk5", [128, D], BF16, n=1)
            ss = sb(st, "ss5", [128, 1], F32, n=2)
            rstd = sb(st, "rstd5", [128, 1], F32, n=2)
            pF = ps(st, "pF", [128, 512], F32, n=2)
            pTl_t = st.enter_context(nc.psum_tensor("pTl", [128, 512], F32))
            pTl = [Buf(pTl_t, f"ptl{k}") for k in range(8)]
            pD = ps(st, "pD", [128, 512], F32, n=4)
            H2v = H2T.rearrange("(kc p) t -> p kc t", p=128)
            tl_i = 0
            for b in range(NB):
                tok0 = b * TB
                lo = max(tok0 - 1, 0); hi = min(tok0 + TB + 1, TT)
                if lo > tok0 - 1:
                    P.op("pool", lambda e: e.memset(h2[:, :, 0:1], 0.0), writes=[h2])
                if hi < tok0 + TB + 1:
                    P.op("pool", lambda e: e.memset(h2[:, :, TB + 1:TB + 2], 0.0), writes=[h2])
                deps = [H2T_B[bb] for bb in range(lo // TB, (hi - 1) // TB + 1)]
                P.dma(h2[:, :, lo - (tok0 - 1):hi - (tok0 - 1)], H2v[:, :, lo:hi], reads=deps, writes=[h2], part=True)
                if tok0 % SL == 0 and tok0 > 0:
                    s = tok0 // SL
                    P.op("pool", lambda e: e.tensor_scalar(h2[:, :, 0:1], h2[:, :, 0:1], carry_t[:, s:s + 1], None, ALU.mult), reads=[carry_t], writes=[h2])
                if (tok0 + TB) % SL == 0 and tok0 + TB < TT:
                    s = (tok0 + TB) // SL
                    P.op("pool", lambda e: e.tensor_scalar(h2[:, :, TB + 1:TB + 2], h2[:, :, TB + 1:TB + 2], carry_t[:, s:s + 1], None, ALU.mult),
                         reads=[carry_t], writes=[h2])
                wcache = {}
                for j in range(44):
                    res = []
                    for half in range(2):
                        ch = half * 44 + j
                        g, jj = ch // 4, ch % 4
                        if g not in wcache:
                            w_ = wup.next()
                            P.dma(w_[:], WS["ffn_up"]["ap"][g], reads=[WS["ffn_up"]["bufs"][g]], writes=[w_])
                            for k_ in [k_ for k_ in wcache if (k_ < 11) == (g < 11)]:
                                del wcache[k_]
                            wcache[g] = w_
                        w_ = wcache[g]
                        p_ = pF.next()
                        for kc in range(16):
                            P.op("pe", lambda e: e.matmul(p_[:], w_[:, kc, jj * 128:(jj + 1) * 128], h2[:, kc, 0:512], start=(kc == 0), stop=(kc == 15)),
                                 reads=[w_, h2], writes=[p_], inc=(kc == 15))
                        tl = pTl[tl_i % 8]; tc = (tl_i % 8) * 64; tl_i += 1
                        for kc in range(16):
                            P.op("pe", lambda e: e.matmul(pTl_t[:, tc:tc + 2], w_[:, kc, jj * 128:(jj + 1) * 128], h2[:, kc, 512:514], start=(kc == 0), stop=(kc == 15)),
                                 reads=[w_, h2], writes=[tl], inc=(kc == 15))
                        u = U.next()
                        P.op("act", lambda e: e.copy(u[:, 0:512], p_[:]), reads=[p_], writes=[u])
                        P.op("dve", lambda e: e.tensor_copy(u[:, 512:514], pTl_t[:, tc:tc + 2]), reads=[tl], writes=[u])
                        cv = (cvg if half == 0 else cvv).next()
                        eng = "dve" if half == 0 else "pool"
                        P.op(eng, lambda e: e.tensor_scalar(cv[:], u[:, 0:TB], fcw[:, ch, 0:1], fcb[:, ch:ch + 1], ALU.mult, ALU.add),
                             reads=[u, fcw, fcb], writes=[cv])
                        for t_ in (1, 2):
                            if eng == "dve":
                                P.op(eng, lambda e: e.scalar_tensor_tensor(cv[:], u[:, t_:t_ + TB], fcw[:, ch, t_:t_ + 1], cv[:], ALU.mult, ALU.add),
                                     reads=[u, fcw], writes=[cv])
                            else:
                                P.op(eng, lambda e: e.tensor_scalar(ctmp5[:], u[:, t_:t_ + TB], fcw[:, ch, t_:t_ + 1], None, ALU.mult),
                                     reads=[u, fcw], writes=[ctmp5])
                                P.op(eng, lambda e: e.tensor_tensor(cv[:], cv[:], ctmp5[:], ALU.add), reads=[ctmp5], writes=[cv])
                        res.append(cv)
                    P.op("act", lambda e: e.activation(res[0][:], res[0][:], AF.Silu), writes=[res[0]])
                    P.op("dve", lambda e: e.tensor_tensor(actT[:, j, :], res[0][:], res[1][:], ALU.mult), reads=[res[0], res[1]], writes=[actT])
                for i in range(4):
                    t0 = tok0 + i * 128
                    P.dma(x1t[i][:], X1[t0:t0 + 128, :], reads=[X1_B[t0 // 128]], writes=[x1t[i]])
                for n in range(4):
                    pds = [pD.next() for i in range(4)]
                    for half in range(2):
                        w_ = wdn.next()
                        P.dma(w_[:], WS["ffn_down"]["ap"][n, :, half * 22:(half + 1) * 22, :], reads=[WS["ffn_down"]["bufs"][n]], writes=[w_])
                        for i in range(4):
                            for kc in range(22):
                                P.op("pe", lambda e: e.matmul(pds[i][:], actT[:, half * 22 + kc, i * 128:(i + 1) * 128], w_[:, kc, :],
                                                              start=(half == 0 and kc == 0), stop=(half == 1 and kc == 21)),
                                     reads=[actT, w_], writes=[pds[i]], inc=(kc == 21))
                    for i in range(4):
                        P.op("dve", lambda e: e.tensor_tensor(x1t[i][:, n * 512:(n + 1) * 512], pds[i][:], x1t[i][:, n * 512:(n + 1) * 512], ALU.add),
                             reads=[pds[i]], writes=[x1t[i]])
                for i in range(4):
                    t0 = tok0 + i * 128
                    s_ = ss.next(); r_ = rstd.next()
                    rms_rstd(x1t[i], s_, r_, junk)
                    P.op("act", lambda e: e.activation(x1t[i][:], x1t[i][:], AF.Copy, scale=r_[:]), reads=[r_], writes=[x1t[i]])
                    P.op("pool", lambda e: e.tensor_tensor(x1t[i][:], x1t[i][:], gfin[:], ALU.mult), reads=[gfin], writes=[x1t[i]])
                    P.dma(y_out[t0:t0 + 128, :], x1t[i][:], reads=[x1t[i]], owner=x1t[i])
            P.barrier()
            P.release(dummy)

    if upto >= 5:
        stage5()

    P.finish()
    return nc, es


def _consts():
    s = np.arange(128)[:, None]
    t = np.arange(128)[None, :]
    same = (s // 64) == (t // 64)
    tri = np.zeros((2, 128, 384), np.float32)
    mask = np.zeros((2, 128, 384), np.float32)
    tri[0, :, 0:128] = CDEC * (same & (s <= t))
    tri[0, :, 128:256] = CDEC * (same & (s < t))
    tri[0, :, 256:384] = CDEC * same
    tri[1, :, 0:128] = CDEC * (same & (s >= t))
    tri[1, :, 128:256] = CDEC * (same & (s > t))
    tri[1, :, 256:384] = CDEC * same
    mask[0, :, 0:128] = same & (s < t)
    mask[0, :, 128:256] = same & (s <= t)
    mask[0, :, 256:384] = same & (s > t)
    mask[1, :, 0:128] = same & (s > t)
    mask[1, :, 128:256] = same & (s >= t)
    mask[1, :, 256:384] = same & (s < t)
    ident = np.eye(128, dtype=np.float32)
    bones = ((s // 64) == (t // 64)).astype(np.float32)
    return tri, mask, ident, bones


def _invcnt(SL, NSLOT, carry):
    TT = SL * NSLOT
    seq_id = np.zeros(TT, np.int64)
    cur = 0
    for s in range(NSLOT):
        if s > 0 and not carry[s]:
            cur += 1
        seq_id[s * SL:(s + 1) * SL] = cur
    out = np.zeros((4, TT), np.float32)
    for sid in np.unique(seq_id):
        idx = np.nonzero(seq_id == sid)[0]
        T = len(idx)
        t = np.arange(T)
        for g, half in enumerate(POOL_HALF):
            lo = np.clip(t - half, 0, T)
            hi = np.clip(t + half, 0, T)
            out[g, idx] = 1.0 / (hi - lo)
    return out


def _pc(v, n):
    return np.ascontiguousarray(np.asarray(v, np.float32).reshape(n, 128).T)


def prep_shared(inp):
    tri, mask, ident, bones = _consts()
    g = lambda k: np.asarray(inp[k], np.float32)
    sh = {
        "w_in": g("w_in")[0], "p_pool": g("p_pool")[0], "p_rwkv": g("p_rwkv")[0], "w_out": g("w_out")[0],
        "ffn_up": g("ffn_up")[0], "ffn_down": g("ffn_down")[0], "pool_w": g("pool_w")[0],
        "w_up": g("w_up")[0].reshape(128, DR), "a_up": g("a_up")[0].reshape(128, DR), "g_up": g("g_up")[0],
        "norm_mix_g": _pc(g("norm_mix_g")[0], 16), "norm_ffn_g": _pc(g("norm_ffn_g")[0], 16),
        "norm_final_g": np.broadcast_to(g("norm_final_g").reshape(1, D), (128, D)),
        "shift_w": np.ascontiguousarray(g("shift_w")[0].reshape(3, 27, 128).transpose(2, 1, 0)),
        "pool_scale": _pc(g("pool_scale")[0], 8), "w0": np.broadcast_to(g("w0")[0][:, None, :], (2, 128, DR)),
        "a0": np.ascontiguousarray(g("a0")[0].reshape(2, 8, 128).transpose(2, 0, 1)),
        "k_k": _pc(g("k_k")[0], 8), "k_a": _pc(g("k_a")[0], 8),
        "r_k": np.ascontiguousarray(g("r_k")[0].reshape(2, 8, 128).transpose(2, 0, 1)),
        "lnx_w": _pc(g("lnx_w")[0], 8), "lnx_b": _pc(g("lnx_b")[0], 8),
        "ffn_conv_w": np.ascontiguousarray(g("ffn_conv_w")[0].reshape(3, 88, 128).transpose(2, 1, 0)),
        "ffn_conv_b": _pc(g("ffn_conv_b")[0], 88),
        "c_tri": tri, "c_mask": mask, "c_ident": ident, "c_bones": bones,
    }
    cm = np.zeros((128, CM_N), np.float32)
    for nm, arr in (("tri0", tri[0]), ("tri1", tri[1]), ("msk0", mask[0]), ("msk1", mask[1]), ("ident", ident), ("bones", bones)):
        o, shp = CM_OFF[nm]
        cm[:, o:o + int(np.prod(shp))] = arr
    cv = np.zeros((128, CV_N), np.float32)
    for nm, (o, shp) in CV_OFF.items():
        if nm == "carry":
            continue
        cv[:, o:o + int(np.prod(shp))] = np.asarray(sh[nm], np.float32).reshape(128, -1)
    out = {k: np.ascontiguousarray(sh[k], dtype=np.float32) for k in
           ("w_in", "p_pool", "p_rwkv", "w_out", "ffn_up", "ffn_down", "pool_w", "w_up", "a_up", "g_up", "norm_final_g", "w0")}
    out["cmat"] = cm
    out["cvec"] = cv
    return out


def core_maps(shared, xs_list, carries, SL, NSLOT):
    maps = []
    for xc, cr in zip(xs_list, carries):
        m = dict(shared)
        m["x"] = np.ascontiguousarray(xc, dtype=np.float32)
        m["invcnt"] = np.ascontiguousarray(np.broadcast_to(_invcnt(SL, NSLOT, cr)[:, None, :], (4, 128, SL * NSLOT)))
        cv = m["cvec"].copy()
        o, _ = CV_OFF["carry"]
        for s in range(1, NSLOT):
            cv[:, o + s] = float(cr[s])
        m["cvec"] = cv
        maps.append(m)
    return maps


def kernel(**inp):
    SL, NSLOT = 2048, 4
    xp = np.asarray(inp["x_prompt"], np.float32)
    xsm = np.asarray(inp["x_sample"], np.float32)
    assign = [[0, 1, 2], [3, 4, 5], [6, 7, 8], [9, 10, 11], [12, 13], [14, 15]]
    xs_list, carries = [], []
    for b in range(2):
        xs_list.append(xp[b])
        carries.append([0, 1, 1, 1])
    for a in assign:
        xc = np.zeros((SL * NSLOT, D), np.float32)
        for s, q in enumerate(a):
            xc[s * SL:(s + 1) * SL] = xsm[q]
        xs_list.append(xc)
        carries.append([0, 0, 0, 0])
    nc, es = build(SL, NSLOT)
    maps = core_maps(prep_shared(inp), xs_list, carries, SL, NSLOT)
    with es:
        res = run_bass_kernel_spmd(nc, maps, core_ids=list(range(NCORES)))
    yp = np.stack([res.results[b]["y"] for b in range(2)], 0)
    ys = np.zeros_like(xsm)
    for ci, a in enumerate(assign):
        yc = res.results[2 + ci]["y"]
        for s, q in enumerate(a):
            ys[q] = yc[s * SL:(s + 1) * SL]
    return (yp.astype(np.float32), ys.astype(np.float32))
```
